# Optimizing a Trainium2 kernel written in Bass

```python
import math
import jax, jax.numpy as jnp
from jax import lax
import numpy as np

D_MODEL = 1024
BATCH = 8
SEQ = 2048
DEPTH = 1

GRID_W = 64
CTX_LEN = 256

N_MOD = 6
NORM_EPS = 1e-6

MIX_WIDTH = D_MODEL
RET_HEADS = 4
RET_WIDTH = MIX_WIDTH // 2
RET_DV = RET_WIDTH // RET_HEADS
RET_DK = RET_DV // 2
RET_CHUNK = 128
DIFF_HEADS = 4
DIFF_WIDTH = MIX_WIDTH - RET_WIDTH
DIFF_DV = DIFF_WIDTH // DIFF_HEADS
DIFF_D = DIFF_DV // 2
Q_BLOCK = 128
ROPE_DIM = RET_DK
ROPE_BASE = 10000.0

RET_QK_COLS = RET_HEADS * RET_DK
IN_SPLITS = (RET_QK_COLS, 2 * RET_QK_COLS, 2 * RET_QK_COLS + RET_WIDTH, 2 * RET_QK_COLS + 2 * RET_WIDTH, 2 * RET_QK_COLS + 2 * RET_WIDTH + DIFF_WIDTH, 2 * RET_QK_COLS + 2 * RET_WIDTH + 2 * DIFF_WIDTH)
IN_COLS = 2 * RET_QK_COLS + 2 * RET_WIDTH + 3 * DIFF_WIDTH

PEER_HEADS = 8
PEER_NKEYS = 128
PEER_N_EXPERTS = PEER_NKEYS * PEER_NKEYS
PEER_QDIM = 256
PEER_HALF = PEER_QDIM // 2
PEER_TOPK = 16
PEER_BLOCK = 128

kernel_name = "hybrid_retention_diffattn_peer_dit_layer"


def rms_norm(x, g, eps=NORM_EPS):
    xf = x.astype(jnp.float32)
    y = xf * lax.rsqrt(jnp.mean(xf * xf, axis=-1, keepdims=True) + eps)
    return (y * g.astype(jnp.float32)).astype(x.dtype)


def modulate(x, g, shift, scale):
    return rms_norm(x, g) * (1.0 + scale) + shift


def adaln_params(cond, w_mod, b_mod):
    mod = jax.nn.silu(cond) @ w_mod + b_mod
    return mod.reshape(cond.shape[0], N_MOD, 1, cond.shape[-1])


def heads(t, n_heads):
    b, l, _ = t.shape
    return t.reshape(b, l, n_heads, -1).transpose(0, 2, 1, 3)


def merge_heads(t):
    b, h, l, e = t.shape
    return t.transpose(0, 2, 1, 3).reshape(b, l, h * e)


def axial_rope_tables(rows, head_dim, dtype):
    quarter = head_dim // 4
    freqs = ROPE_BASE ** (-jnp.arange(quarter, dtype=jnp.float32) / quarter)
    row = jnp.repeat(jnp.arange(rows, dtype=jnp.float32), GRID_W)
    col = jnp.tile(jnp.arange(GRID_W, dtype=jnp.float32), rows)
    ar = row[:, None] * freqs
    ac = col[:, None] * freqs
    ang = jnp.concatenate([ar, ar, ac, ac], axis=-1)
    return jnp.cos(ang).astype(dtype), jnp.sin(ang).astype(dtype)


def apply_rope(x, cos, sin):
    quarter = x.shape[-1] // 4
    xr = x.reshape(x.shape[:-1] + (2, 2, quarter))
    rot = jnp.concatenate([-xr[..., 1:, :], xr[..., :1, :]], axis=-2).reshape(x.shape)
    return x * cos + rot * sin


def decayed_state(k, v, log_gamma):
    l = k.shape[2]
    dist = (l - 1 - jnp.arange(l)).astype(jnp.float32)
    w = jnp.exp(dist[None, :] * log_gamma[:, None]).astype(k.dtype)
    return jnp.einsum('bhjd,hj,bhje->bhde', k, w, v)


def retention_chunkwise(q, k, v, log_gamma, init_state, strict):
    b, h, l, dk = q.shape
    dv = v.shape[-1]
    n = l // RET_CHUNK
    C = RET_CHUNK
    qc = q.reshape(b, h, n, C, dk)
    kc = k.reshape(b, h, n, C, dk)
    vc = v.reshape(b, h, n, C, dv)
    idx = jnp.arange(C, dtype=jnp.float32)
    diff = idx[:, None] - idx[None, :]
    mask = (diff > 0) if strict else (diff >= 0)
    lg = log_gamma[:, None, None]
    intra_decay = jnp.where(mask, jnp.exp(jnp.where(mask, diff, 0.0) * lg), 0.0).astype(q.dtype)
    scores = jnp.einsum('bhnid,bhnjd->bhnij', qc, kc) * intra_decay[None, :, None]
    out = jnp.einsum('bhnij,bhnje->bhnie', scores, vc)
    k_w = jnp.exp((C - 1 - idx)[None, :] * log_gamma[:, None]).astype(q.dtype)
    local = jnp.einsum('bhnjd,hj,bhnje->bhnde', kc, k_w, vc)
    chunk_decay = jnp.exp(C * log_gamma).astype(q.dtype)[None, :, None, None]

    def step(r, s):
        return chunk_decay * r + s, r

    _, r_prev = lax.scan(step, init_state.astype(local.dtype), jnp.moveaxis(local, 2, 0))
    r_prev = jnp.moveaxis(r_prev, 0, 2)
    q_w = jnp.exp((idx + 1.0)[None, :] * log_gamma[:, None]).astype(q.dtype)
    out = out + jnp.einsum('bhnid,hi,bhnde->bhnie', qc, q_w, r_prev)
    return out.reshape(b, h, l, dv)


def bidirectional_retention(q, k, v, lg_f, lg_b, state_f, state_b):
    flip = lambda t: jnp.flip(t, axis=2)
    y_f = retention_chunkwise(q, k, v, lg_f, state_f, strict=False)
    y_b = flip(retention_chunkwise(flip(q), flip(k), flip(v), lg_b, state_b, strict=True))
    return y_f + y_b


def retention_group(rq, rk, rv, rg, crq, crk, crv, crg, decay_logit, norm_g, cos, sin, with_ctx_out):
    scale = RET_DK ** -0.5
    log_gamma = jax.nn.log_sigmoid(decay_logit.astype(jnp.float32))
    lg_f, lg_b = log_gamma[0], log_gamma[1]
    q = apply_rope(heads(rq, RET_HEADS), cos, sin)
    k = apply_rope(heads(rk, RET_HEADS), cos, sin) * scale
    v = heads(rv, RET_HEADS)
    kc = heads(crk, RET_HEADS) * scale
    vc = heads(crv, RET_HEADS)
    state_f = decayed_state(kc, vc, lg_f)
    state_b = decayed_state(jnp.flip(kc, axis=2), jnp.flip(vc, axis=2), lg_b)
    y = bidirectional_retention(q, k, v, lg_f, lg_b, state_f, state_b)
    g_norm = norm_g[:, None, :]
    out = merge_heads(rms_norm(y, g_norm)) * jax.nn.silu(rg)
    ctx_out = None
    if with_ctx_out:
        qc = heads(crq, RET_HEADS)
        zero = jnp.zeros(state_f.shape, state_f.dtype)
        yc = bidirectional_retention(qc, kc, vc, lg_f, lg_b, zero, zero)
        ctx_out = merge_heads(rms_norm(yc, g_norm)) * jax.nn.silu(crg)
    return out, ctx_out


def diff_softmax_attend(q, k, v, lam):
    s = jnp.einsum('bhpqd,bhpkd->bhpqk', q, k).astype(jnp.float32) * (DIFF_D ** -0.5)
    p = jax.nn.softmax(s, axis=-1)
    a = (p[:, :, 0] - lam * p[:, :, 1]).astype(v.dtype)
    return jnp.einsum('bhqk,bhke->bhqe', a, v)


def diff_attention_group(dq, dk, dv, cdq, cdk, cdv, qk_norm_g, lam_params, norm_g, lam_init, cos, sin, with_ctx_out):
    def qk_heads(t, g):
        b_, l_, _ = t.shape
        t = t.reshape(b_, l_, DIFF_HEADS, 2, DIFF_D).transpose(0, 2, 3, 1, 4)
        return rms_norm(t, g)

    q = apply_rope(qk_heads(dq, qk_norm_g[0]), cos, sin)
    k = apply_rope(qk_heads(dk, qk_norm_g[1]), cos, sin)
    v = heads(dv, DIFF_HEADS)
    kc = qk_heads(cdk, qk_norm_g[1])
    vc = heads(cdv, DIFF_HEADS)
    lp = lam_params.astype(jnp.float32)
    lam = jnp.exp(jnp.sum(lp[0] * lp[1])) - jnp.exp(jnp.sum(lp[2] * lp[3])) + lam_init
    keys = jnp.concatenate([kc, k], axis=3)
    vals = jnp.concatenate([vc, v], axis=2)
    b, h, _, l, d = q.shape
    nb = l // Q_BLOCK
    q_blocks = jnp.moveaxis(q.reshape(b, h, 2, nb, Q_BLOCK, d), 3, 0)
    y = lax.map(lambda qb: diff_softmax_attend(qb, keys, vals, lam), q_blocks)
    y = jnp.moveaxis(y, 0, 2).reshape(b, h, l, DIFF_DV)
    g_norm = norm_g[:, None, :]
    out = merge_heads(rms_norm(y, g_norm) * (1.0 - lam_init))
    ctx_out = None
    if with_ctx_out:
        qc = qk_heads(cdq, qk_norm_g[0])
        yc = diff_softmax_attend(qc, kc, vc, lam)
        ctx_out = merge_heads(rms_norm(yc, g_norm) * (1.0 - lam_init))
    return out, ctx_out


def peer_ffn(h, w_query, sub_keys, expert_u, expert_v):
    b, l, d = h.shape
    tokens = h.reshape(-1, PEER_BLOCK, d)

    def block_fn(hb):
        t = hb.shape[0]
        q = (hb @ w_query).reshape(t, PEER_HEADS, 2, PEER_HALF)
        s = jnp.einsum('thpc,hpnc->thpn', q, sub_keys).astype(jnp.float32)
        v1, i1 = lax.top_k(s[:, :, 0], PEER_TOPK)
        v2, i2 = lax.top_k(s[:, :, 1], PEER_TOPK)
        cand = (v1[..., :, None] + v2[..., None, :]).reshape(t, PEER_HEADS, PEER_TOPK * PEER_TOPK)
        cand_idx = (i1[..., :, None] * PEER_NKEYS + i2[..., None, :]).reshape(t, PEER_HEADS, PEER_TOPK * PEER_TOPK)
        best, pos = lax.top_k(cand, PEER_TOPK)
        experts = jnp.take_along_axis(cand_idx, pos, axis=-1)
        gate = jax.nn.softmax(best, axis=-1).astype(hb.dtype)
        u = expert_u[experts]
        vsel = expert_v[experts]
        act = jax.nn.gelu(jnp.einsum('thkd,td->thk', u, hb), approximate=False)
        return jnp.einsum('thk,thkd->td', gate * act, vsel)

    return lax.map(block_fn, tokens).reshape(b, l, d)


def setup_inputs(seed: int = 0) -> dict:
    key = jax.random.key(seed)
    ks = jax.random.split(key, 20)
    nrm = lambda k, shape, s: jax.random.normal(k, shape, jnp.float32) * s
    L = DEPTH
    D = D_MODEL
    decay_init = jnp.asarray(np.log(2.0 ** (5 + np.arange(RET_HEADS)) - 1.0).astype(np.float32))
    return {
        "x": nrm(ks[0], (BATCH, SEQ, D), 1.0),
        "c": nrm(ks[1], (BATCH, D), 1.0),
        "ctx": nrm(ks[2], (BATCH, CTX_LEN, D), 1.0),
        "c_ctx": nrm(ks[3], (D,), 1.0),
        "w_mod": nrm(ks[4], (L, D, N_MOD * D), 0.5 * D ** -0.5),
        "b_mod": nrm(ks[5], (L, N_MOD * D), 0.02),
        "norm1_g": 1.0 + nrm(ks[6], (L, D), 0.02),
        "norm2_g": 1.0 + nrm(ks[7], (L, D), 0.02),
        "w_in": nrm(ks[8], (L, D, IN_COLS), D ** -0.5),
        "ret_decay_logit": decay_init[None, None, :] + nrm(ks[9], (L, 2, RET_HEADS), 0.1),
        "ret_norm_g": 1.0 + nrm(ks[10], (L, RET_HEADS, RET_DV), 0.02),
        "diff_qk_norm_g": 1.0 + nrm(ks[11], (L, 2, DIFF_D), 0.02),
        "diff_lambda": nrm(ks[12], (L, 4, DIFF_D), 0.1),
        "diff_norm_g": 1.0 + nrm(ks[13], (L, DIFF_HEADS, DIFF_DV), 0.02),
        "w_out": nrm(ks[14], (L, MIX_WIDTH, D), MIX_WIDTH ** -0.5),
        "peer_w_query": nrm(ks[15], (L, D, PEER_HEADS * PEER_QDIM), D ** -0.5),
        "peer_sub_keys": nrm(ks[16], (L, PEER_HEADS, 2, PEER_NKEYS, PEER_HALF), PEER_HALF ** -0.5),
        "peer_u": nrm(ks[17], (L, PEER_N_EXPERTS, D), D ** -0.5),
        "peer_v": nrm(ks[18], (L, PEER_N_EXPERTS, D), (PEER_HEADS * PEER_TOPK) ** -0.5),
    }


def reference(x, c, ctx, c_ctx, w_mod, b_mod, norm1_g, norm2_g, w_in, ret_decay_logit, ret_norm_g, diff_qk_norm_g, diff_lambda, diff_norm_g, w_out, peer_w_query, peer_sub_keys, peer_u, peer_v):
    rows = x.shape[1] // GRID_W
    cos, sin = axial_rope_tables(rows, ROPE_DIM, x.dtype)
    for layer in range(DEPTH):
        last = layer == DEPTH - 1
        lam_init = 0.8 - 0.6 * math.exp(-0.3 * layer)
        mod = adaln_params(c, w_mod[layer], b_mod[layer])
        mod_c = adaln_params(c_ctx[None, :], w_mod[layer], b_mod[layer])
        h = modulate(x, norm1_g[layer], mod[:, 0], mod[:, 1])
        hc = modulate(ctx, norm1_g[layer], mod_c[:, 0], mod_c[:, 1])
        rq, rk, rv, rg, dq, dk, dv = jnp.split(h @ w_in[layer], IN_SPLITS, axis=-1)
        crq, crk, crv, crg, cdq, cdk, cdv = jnp.split(hc @ w_in[layer], IN_SPLITS, axis=-1)
        ret_lat, ret_ctx = retention_group(rq, rk, rv, rg, crq, crk, crv, crg, ret_decay_logit[layer], ret_norm_g[layer], cos, sin, not last)
        diff_lat, diff_ctx = diff_attention_group(dq, dk, dv, cdq, cdk, cdv, diff_qk_norm_g[layer], diff_lambda[layer], diff_norm_g[layer], lam_init, cos, sin, not last)
        x = x + mod[:, 2] * (jnp.concatenate([ret_lat, diff_lat], axis=-1) @ w_out[layer])
        h2 = modulate(x, norm2_g[layer], mod[:, 3], mod[:, 4])
        x = x + mod[:, 5] * peer_ffn(h2, peer_w_query[layer], peer_sub_keys[layer], peer_u[layer], peer_v[layer])
        if not last:
            ctx = ctx + mod_c[:, 2] * (jnp.concatenate([ret_ctx, diff_ctx], axis=-1) @ w_out[layer])
            hc2 = modulate(ctx, norm2_g[layer], mod_c[:, 3], mod_c[:, 4])
            ctx = ctx + mod_c[:, 5] * peer_ffn(hc2, peer_w_query[layer], peer_sub_keys[layer], peer_u[layer], peer_v[layer])
    return x
```

```python
import math
from contextlib import ExitStack

import numpy as np
import concourse.bass as bass
import concourse.mybir as mybir
from concourse.bass_utils import run_bass_kernel_spmd

F32 = mybir.dt.float32
BF16 = mybir.dt.bfloat16
I32 = mybir.dt.int32
U32 = mybir.dt.uint32
U8 = mybir.dt.uint8
AF = mybir.ActivationFunctionType
ALU = mybir.AluOpType
AX = mybir.AxisListType
DTB = {F32: 4, BF16: 2, I32: 4, U32: 4, U8: 1}

D = 1024
L = 2048
LC = 256
NT = L // 128
NTC = LC // 128
NK = NT + NTC
EPS = 1e-6
LAM_INIT = 0.8 - 0.6 * math.exp(-0.3 * 0)
NEXP = 16384


class Clock:
    def __init__(self, name, step):
        self.name, self.step, self.count, self.sem = name, step, 0, None


class Buf:
    def __init__(self, name=""):
        self.name = name
        self.w = None
        self.r = {}


class FW:
    ENGS = ("sync", "scalar", "vector", "gpsimd", "tensor")

    def __init__(self, nc):
        self.nc = nc
        self.streams = {e: [] for e in self.ENGS}
        self.eclk = {e: Clock("e_" + e, 1) for e in self.ENGS}
        self.clocks = list(self.eclk.values())
        self.seen = {e: {} for e in self.ENGS}
        self.pend_waits = {e: {} for e in self.ENGS}
        self.pend_rw = {e: ([], []) for e in self.ENGS}
        self.nops = 0

    def dma_clock(self, name):
        c = Clock("d_" + name, 16)
        self.clocks.append(c)
        return c

    def _need(self, eng, clk, val, waits):
        if self.seen[eng].get(clk, 0) >= val:
            return
        self.seen[eng][clk] = val
        waits[clk] = max(waits.get(clk, 0), val)

    def fence(self, skip=()):
        for e in self.ENGS:
            r, w = self.pend_rw[e]
            assert not r and not w, "fence with unpublished ops on " + e
            for c in self.clocks:
                if c in skip:
                    continue
                if c.count > 0 and self.seen[e].get(c, 0) < c.count:
                    self.seen[e][c] = c.count
                    self.pend_waits[e][c] = c.count

    def op(self, eng, fn, reads=(), writes=(), clock=None, inc=True):
        if getattr(self, "dry", False):
            return None
        own = self.eclk[eng]
        clk = clock if clock is not None else own
        waits = self.pend_waits[eng]
        self.pend_waits[eng] = {}
        for b in reads:
            if b.w is not None:
                c, v = b.w
                if c == "pending":
                    assert v == eng, f"read of unpublished buffer {b.name} from {eng}"
                    continue
                if c is own and eng == "tensor":
                    continue
                self._need(eng, c, v, waits)
        same_ok = (eng == "tensor")
        for b in writes:
            if b.w is not None:
                c, v = b.w
                if c == "pending":
                    assert v == eng, f"write of unpublished buffer {b.name} from {eng}"
                elif c is not own or not same_ok:
                    self._need(eng, c, v, waits)
            for c, v in b.r.items():
                if c is not own or not same_ok:
                    self._need(eng, c, v, waits)
        if clock is not None and clock.count > 0:
            self._need(eng, clock, clock.count, waits)
        if not inc:
            assert clock is None
            pr, pw = self.pend_rw[eng]
            pr.extend(reads)
            pw.extend(writes)
            for b in writes:
                b.w = ("pending", eng)
                b.r = {}
            self.streams[eng].append((list(waits.items()), fn, None))
            self.nops += 1
            return None
        clk.count += clk.step
        val = clk.count
        allr, allw = list(reads), list(writes)
        if clock is None:
            pr, pw = self.pend_rw[eng]
            allr += pr
            allw += pw
            self.pend_rw[eng] = ([], [])
        for b in allr:
            b.r[clk] = val
        for b in allw:
            b.w = (clk, val)
            b.r = {}
        self.streams[eng].append((list(waits.items()), fn, clk))
        self.nops += 1
        return val

    def emit(self):
        nc = self.nc
        for e in self.ENGS:
            r, w = self.pend_rw[e]
            assert not r and not w, "unpublished ops at end on " + e
        with ExitStack() as es:
            for c in self.clocks:
                c.sem = es.enter_context(nc.semaphore(c.name))
            block = es.enter_context(nc.Block())
            fw = self

            def run(eng_name):
                def body(eng):
                    for waits, fn, clk in fw.streams[eng_name]:
                        for c, v in waits:
                            eng.wait_ge(c.sem, v)
                        ins = fn(eng)
                        if clk is not None:
                            ins.then_inc(clk.sem, clk.step)
                    if eng_name == "sync":
                        for c in fw.clocks:
                            if c.count > 0:
                                eng.wait_ge(c.sem, c.count)
                return body

            block.sync(run("sync"))
            block.scalar(run("scalar"))
            block.vector(run("vector"))
            block.gpsimd(run("gpsimd"))
            block.tensor(run("tensor"))


class Alloc:
    def __init__(self, arena, base, limit):
        self.arena, self.off, self.limit = arena, base, limit

    def get(self, shape, dt, parts=128):
        n = int(np.prod(shape)) * DTB[dt]
        n = (n + 31) // 32 * 32
        assert self.off + n <= self.limit, f"arena overflow {self.off}+{n} > {self.limit}"
        v = self.arena[0:parts, self.off:self.off + n].bitcast(dt)
        self.off += n
        if len(shape) == 1:
            return v[:, 0:shape[0]]
        names = " ".join(f"a{i}" for i in range(len(shape)))
        kw = {f"a{i}": s for i, s in enumerate(shape)}
        v = v[:, 0:int(np.prod(shape))]
        if len(shape) > 1:
            v = v.rearrange(f"p ({names}) -> p {names}", **kw)
        return v


KBY = 1024


class Builder:
    def __init__(self, stop_after=None, taps=()):
        self.stop_after = stop_after
        self.want_taps = set(taps)
        self.taps = {}
        self.nc = nc = bass.Bass("TRN2", target_bir_lowering=False)
        self.f = FW(nc)
        self.es = ExitStack()
        di = lambda name, shape, dt=F32: nc.dram_tensor(name, list(shape), dt, kind="ExternalInput").ap()
        self.x = di("x", [L, D]); self.ctx = di("ctx", [LC, D])
        self.c = di("c", [D]); self.c_ctx = di("c_ctx", [D])
        self.w_mod = di("w_mod", [D, 6 * D]); self.b_mod = di("b_mod", [6 * D])
        self.norm1_g = di("norm1_g", [D]); self.norm2_g = di("norm2_g", [D])
        self.w_in = di("w_in", [D, 3072])
        self.decay = di("decay", [8]); self.ret_norm_g = di("ret_norm_g", [4, 128])
        self.qk_g = di("qk_g", [128]); self.dlam = di("dlam", [256]); self.diff_norm_g = di("diff_norm_g", [4, 128])
        self.w_out = di("w_out", [D, D]); self.wq = di("wq", [D, 2048]); self.sk = di("sk", [16, 128, 128])
        self.pu = di("pu", [NEXP, D]); self.pv = di("pv", [NEXP, D])
        self.rcos = di("rcos", [L, 64]); self.rsin = di("rsin", [L, 64])
        self.out = nc.dram_tensor("out", [L, D], F32, kind="ExternalOutput").ap()
        self.puv = nc.dram_tensor("puv", [NEXP, 2 * D], BF16, kind="Internal").ap()
        self.Bpuv = Buf("puv")
        self.arena = self.es.enter_context(nc.sbuf_tensor("arena", [128, 207 * KBY], U8))
        self.psum = self.es.enter_context(nc.psum_tensor("psum", [128, 8, 2048], U8))
        self.ld = [self.f.dma_clock(f"ld{i}") for i in range(16)]
        self.ldi = 0
        self.stc = self.f.dma_clock("st")
        self.Bout = Buf("out")
        self.cv_clocks = []

    def pbank(self, b, dt=F32, nb=1):
        if nb == 1:
            return self.psum[:, b, :].bitcast(dt)
        return self.psum[:, b:b + nb, :].bitcast(dt)

    def E(self, eng, method, reads, writes, *a, inc=True, clock=None, **kw):
        return self.f.op(eng, lambda e: getattr(e, method)(*a, **kw), reads=reads, writes=writes, clock=clock, inc=inc)

    def V(self, method, reads, writes, *a, **kw):
        return self.E("vector", method, reads, writes, *a, **kw)

    def G(self, method, reads, writes, *a, **kw):
        return self.E("gpsimd", method, reads, writes, *a, **kw)

    def A(self, reads, writes, out, in_, func, **kw):
        return self.E("scalar", "activation", reads, writes, out=out, in_=in_, func=func, **kw)

    def MM(self, reads, writes, out, lhsT, rhs, start, stop, inc=True):
        return self.E("tensor", "matmul", reads, writes, out, lhsT=lhsT, rhs=rhs, start=start, stop=stop, inc=inc)

    def TR(self, reads, writes, out, in_, ident, inc=True):
        return self.E("tensor", "transpose", reads, writes, out=out, in_=in_, identity=ident, inc=inc)

    def dma(self, out, in_, reads=(), writes=(), eng="sync", clock=None):
        if clock is None:
            clock = self.ld[self.ldi % len(self.ld)]
            self.ldi += 1
        return self.E(eng, "dma_start", reads, writes, out=out, in_=in_, clock=clock)

    def tap(self, name, ap, buf):
        if name not in self.want_taps:
            return
        shape = list(ap.shape)
        o = self.nc.dram_tensor("tap_" + name, shape, ap.dtype, kind="ExternalOutput").ap()
        self.taps[name] = o
        self.dma(o, ap, reads=[buf], writes=[Buf()], clock=self.stc)

    def rsqrt_chain(self, src, dst, tmp, n_inv, Bsrc, Bdst, Btmp):
        self.V("tensor_scalar", [Bsrc], [Btmp], out=tmp, in0=src, scalar1=n_inv, scalar2=EPS, op0=ALU.mult, op1=ALU.add)
        self.A([Btmp], [Btmp], tmp, tmp, AF.Sqrt)
        self.V("reciprocal", [Btmp], [Bdst], out=dst, in_=tmp)

    def rsqrt_lnexp(self, src, dst, tmp, n_inv, Bsrc, Bdst, Btmp):
        self.V("tensor_scalar", [Bsrc], [Btmp], out=tmp, in0=src, scalar1=n_inv, scalar2=EPS, op0=ALU.mult, op1=ALU.add)
        self.A([Btmp], [Btmp], tmp, tmp, AF.Ln)
        self.A([Btmp], [Bdst], dst, tmp, AF.Exp, scale=-0.5)

    def build(self):
        nc, f = self.nc, self.f
        AR = self.arena
        P = Alloc(AR, 0, 22 * KBY)
        self.ident_f = P.get([128], F32); self.ident_b = P.get([128], BF16)
        self.ones_f = P.get([128], F32); self.ones_b = P.get([128], BF16)
        self.modp = P.get([4 * D], F32)
        self.lg = P.get([8], F32)
        self.nlgb = P.get([4], F32)
        self.cshift = P.get([8], F32)
        self.rg_col = P.get([4], F32)
        self.dg_col = P.get([4], F32)
        self.nlam = P.get([1], F32)
        self.gqk = P.get([1024], F32)
        self.Bc = Buf("consts")
        self.Bmodp = Buf("modp")
        self.Bsmall = Buf("small")
        self.setup_consts(P)
        self.phaseA()
        if self.stop_after == "A":
            return self.finish()
        self.phaseB()
        if self.stop_after == "B":
            return self.finish()
        self.phaseC()
        if self.stop_after == "C":
            return self.finish()
        self.phaseD()
        if self.stop_after == "D":
            return self.finish()
        self.phaseE()
        if self.stop_after == "E":
            return self.finish()
        self.phaseF()
        if self.stop_after == "F":
            return self.finish()
        self.phaseG()
        if self.stop_after == "G":
            return self.finish()
        self.phaseH()
        return self.finish()

    def finish(self):
        self.f.emit()
        self.es.close()
        return self.nc

    def setup_consts(self, P):
        Bc = self.Bc
        self.G("memset", [], [Bc], self.ones_f, 1.0)
        self.G("memset", [], [Bc], self.ones_b, 1.0)
        self.G("affine_select", [Bc], [Bc], out=self.ident_f, in_=self.ones_f, pattern=[[-1, 128]],
               compare_op=ALU.is_equal, fill=0.0, base=0, channel_multiplier=1)
        self.V("tensor_copy", [Bc], [Bc], out=self.ident_b, in_=self.ident_f)
        Bs = self.Bsmall
        T = Alloc(self.arena, 200 * KBY, 207 * KBY)
        dl = T.get([8], F32); e1 = T.get([8], F32)
        self.dma(dl, self.decay.partition_broadcast(128), writes=[Bs])
        self.A([Bs], [Bs], e1, dl, AF.Exp, scale=-1.0)
        self.A([Bs], [Bs], e1, e1, AF.Ln, bias=1.0)
        self.V("tensor_scalar", [Bs], [Bs], out=self.lg, in0=e1, scalar1=-1.0, scalar2=None, op0=ALU.mult)
        self.V("tensor_copy", [Bs], [Bs], out=self.nlgb, in_=e1[:, 4:8])
        self.A([Bs], [Bs], self.cshift[:, 0:4], self.lg[:, 0:4], AF.Exp, scale=-128.0)
        self.A([Bs], [Bs], self.cshift[:, 4:8], self.lg[:, 4:8], AF.Exp, scale=128.0)
        for h in range(4):
            self.dma(self.rg_col[:, h:h + 1], self.ret_norm_g[h, :].rearrange("(p o) -> p o", o=1), writes=[Bs])
            self.dma(self.dg_col[:, h:h + 1], self.diff_norm_g[h, :].rearrange("(p o) -> p o", o=1), writes=[Bs])
        self.V("tensor_scalar", [Bs], [Bs], out=self.dg_col, in0=self.dg_col, scalar1=1.0 - LAM_INIT, scalar2=None, op0=ALU.mult)
        lamb = T.get([256], F32); pr = T.get([128], F32); s2 = T.get([2], F32)
        self.dma(lamb, self.dlam.partition_broadcast(128), writes=[Bs])
        lv = lamb.rearrange("p (a b d) -> p a b d", a=2, b=2)
        self.V("tensor_tensor", [Bs], [Bs], out=pr.rearrange("p (a d) -> p a d", a=2), in0=lv[:, :, 0, :], in1=lv[:, :, 1, :], op=ALU.mult)
        self.V("tensor_reduce", [Bs], [Bs], out=s2, in_=pr.rearrange("p (a d) -> p a d", a=2), axis=AX.X, op=ALU.add)
        self.A([Bs], [Bs], s2, s2, AF.Exp)
        self.V("tensor_tensor", [Bs], [Bs], out=self.nlam, in0=s2[:, 1:2], in1=s2[:, 0:1], op=ALU.subtract)
        self.V("tensor_scalar", [Bs], [Bs], out=self.nlam, in0=self.nlam, scalar1=-LAM_INIT, scalar2=None, op0=ALU.add)
        gq = T.get([128], F32)
        self.dma(gq, self.qk_g.partition_broadcast(128), writes=[Bs])
        gv = self.gqk.rearrange("p (a n d) -> p a n d", a=2, n=8)
        self.V("tensor_copy", [Bs], [Bs], out=gv, in_=gq.rearrange("p (a d) -> p a d", a=2).unsqueeze(2).to_broadcast([128, 2, 8, 64]))
        self.tap("lg", self.lg, Bs)
        self.tap("nlam", self.nlam, Bs)

    def phaseA(self):
        f = self.f
        S = Alloc(self.arena, 90 * KBY, 207 * KBY)
        self.modA = S.get([2 * D], F32)
        self.modC = S.get([2 * D], F32)
        self.BmodA, self.BmodC = Buf("modA"), Buf("modC")
        self.phaseB_base = S.off
        ccol = S.get([8], F32); cccol = S.get([8], F32)
        screp = S.get([8, 128], F32); sccrep = S.get([8, 128], F32)
        bmr = S.get([6 * D], F32, parts=1)
        ones1 = self.ones_f[0:1, :]
        n1b = S.get([D], F32); n2b = S.get([D], F32)
        stg = [S.get([8, 512], F32) for _ in range(2)]
        Bst = [Buf("stg0"), Buf("stg1")]
        Bcc, Brep, Bbm, Bn = Buf("ccol"), Buf("rep"), Buf("bmr"), Buf("nb")
        for k in range(8):
            self.dma(ccol[:, k:k + 1], self.c[k * 128:(k + 1) * 128].rearrange("(p o) -> p o", o=1), writes=[Bcc])
            self.dma(cccol[:, k:k + 1], self.c_ctx[k * 128:(k + 1) * 128].rearrange("(p o) -> p o", o=1), writes=[Bcc])
        self.dma(bmr, self.b_mod.rearrange("(o n) -> o n", o=1), writes=[Bbm])
        self.dma(n1b, self.norm1_g.partition_broadcast(128), writes=[Bn])
        self.dma(n2b, self.norm2_g.partition_broadcast(128), writes=[Bn])
        self.A([Bcc], [Bcc], ccol, ccol, AF.Silu)
        self.A([Bcc], [Bcc], cccol, cccol, AF.Silu)
        self.V("tensor_copy", [Bcc], [Brep], out=screp, in_=ccol.unsqueeze(2).to_broadcast([128, 8, 128]))
        self.V("tensor_copy", [Bcc], [Brep], out=sccrep, in_=cccol.unsqueeze(2).to_broadcast([128, 8, 128]))
        Bps = [Buf(f"psA{i}") for i in range(4)]
        wm = self.w_mod.rearrange("(k p) n -> p k n", p=128)
        for g in range(12):
            s = g % 2
            self.dma(stg[s], wm[:, :, g * 512:(g + 1) * 512], writes=[Bst[s]])
            for which in range(2 if g < 4 else 1):
                rep = screp if which == 0 else sccrep
                pb = self.pbank(2 * which + s)
                Bp = Bps[2 * which + s]
                for dc in range(8):
                    self.MM([Brep, Bst[s], self.Bc], [Bp], pb, rep[:, dc, :], stg[s][:, dc, :], dc == 0, False, inc=False)
                self.MM([Bbm, self.Bc], [Bp], pb, ones1, bmr[0:1, g * 512:(g + 1) * 512], False, True)
                if which == 1:
                    dst, Bd = self.modC[:, g * 512:(g + 1) * 512], self.BmodC
                elif g < 4:
                    dst, Bd = self.modA[:, g * 512:(g + 1) * 512], self.BmodA
                else:
                    dst, Bd = self.modp[:, (g - 4) * 512:(g - 3) * 512], self.Bmodp
                self.A([Bp], [Bd], dst, pb, AF.Copy)
        for m, Bm, nb in ((self.modA, self.BmodA, n1b), (self.modC, self.BmodC, n1b)):
            self.V("scalar_tensor_tensor", [Bm, Bn], [Bm], out=m[:, D:2 * D], in0=m[:, D:2 * D], scalar=1.0, in1=nb, op0=ALU.add, op1=ALU.mult)
        self.V("scalar_tensor_tensor", [self.Bmodp, Bn], [self.Bmodp], out=self.modp[:, 2 * D:3 * D], in0=self.modp[:, 2 * D:3 * D], scalar=1.0, in1=n2b, op0=ALU.add, op1=ALU.mult)
        self.tap("modA", self.modA, self.BmodA)
        self.tap("modC", self.modC, self.BmodC)
        self.tap("modp", self.modp, self.Bmodp)
        f.fence()

    def phaseB(self):
        f = self.f
        self.hT = Alloc(self.arena, 54 * KBY, 90 * KBY).get([8, NK * 128], BF16)
        self.BhT = [Buf(f"hT{t}") for t in range(NK)]
        S = Alloc(self.arena, self.phaseB_base, 207 * KBY)
        xin = [S.get([D], F32) for _ in range(2)]
        tmp = [S.get([D], F32) for _ in range(2)]
        hb = [S.get([D], BF16) for _ in range(2)]
        junk = S.get([D], F32)
        ss = S.get([NK], F32); vv = S.get([NK], F32); rstd = S.get([NK], F32)
        Bx = [Buf(), Buf()]; Bt = [Buf(), Buf()]; Bh = [Buf(), Buf()]; Bj = Buf(); Bss = [Buf() for _ in range(NK)]
        Bps = [Buf(), Buf()]
        def b_stage1(tt):
            s = tt % 2
            src = self.ctx[tt * 128:(tt + 1) * 128, :] if tt < NTC else self.x[(tt - NTC) * 128:(tt - NTC + 1) * 128, :]
            self.dma(xin[s], src, writes=[Bx[s]])
            self.A([Bx[s]], [Bj, Bss[tt]], junk, xin[s], AF.Square, accum_out=ss[:, tt:tt + 1])
            self.rsqrt_chain(ss[:, tt:tt + 1], rstd[:, tt:tt + 1], vv[:, tt:tt + 1], 1.0 / D, Bss[tt], Bss[tt], Bss[tt])

        def b_stage2(tt):
            s = tt % 2
            mod, Bm = (self.modC, self.BmodC) if tt < NTC else (self.modA, self.BmodA)
            self.V("scalar_tensor_tensor", [Bx[s], Bss[tt], Bm], [Bt[s]], out=tmp[s], in0=xin[s], scalar=rstd[:, tt:tt + 1], in1=mod[:, D:2 * D], op0=ALU.mult, op1=ALU.mult)
            self.G("tensor_tensor", [Bt[s], Bm], [Bh[s]], out=hb[s], in0=tmp[s], in1=mod[:, 0:D], op=ALU.add)
            pt = self.pbank(s, BF16).rearrange("p (k t) -> p k t", k=8)
            for dc in range(8):
                self.TR([Bh[s], self.Bc], [Bps[s]], pt[:, dc, :], hb[s][:, dc * 128:(dc + 1) * 128], self.ident_b, inc=(dc == 7))
            self.A([Bps[s]], [self.BhT[tt]], self.hT[:, :, tt * 128:(tt + 1) * 128], pt, AF.Copy)

        b_stage1(0)
        for tt in range(NK):
            if tt + 1 < NK:
                b_stage1(tt + 1)
            b_stage2(tt)
        self.tap("hT", self.hT, self.BhT[NK - 1])
        f.fence()

    def load_weight_bf16(self, dst, Bdst, src, c0, ncols, S, chunk=256, cast_engs=("gpsimd", "scalar"), stg=None, Bst=None):
        if stg is None:
            stg = [S.get([8, chunk], F32) for _ in range(2)]
            Bst = [Buf(), Buf()]
        srcv = src.rearrange("(k p) n -> p k n", p=128)
        for i in range(ncols // chunk):
            s = i % 2
            self.dma(stg[s], srcv[:, :, c0 + i * chunk:c0 + (i + 1) * chunk], writes=[Bst[s]])
            eng = cast_engs[i % len(cast_engs)]
            if eng == "scalar":
                self.A([Bst[s]], [Bdst], dst[:, :, i * chunk:(i + 1) * chunk], stg[s], AF.Copy)
            else:
                self.E(eng, "tensor_copy", [Bst[s]], [Bdst], out=dst[:, :, i * chunk:(i + 1) * chunk], in_=stg[s])

    def rope(self, eng_a, eng_b, src, Bsrc, dst, Bdst, t1, t2, Bt1, Bt2, cs, sn, Bcs, nblk):
        v3 = lambda a: a.rearrange("p (n d) -> p n d", n=nblk)
        v5 = lambda a: a.rearrange("p (n g s q) -> p n g s q", n=nblk, g=2, s=2)
        cb = cs.unsqueeze(1).to_broadcast([128, nblk, 64])
        s5 = sn.rearrange("p (g s q) -> p g s q", g=2, s=2).unsqueeze(1).to_broadcast([128, nblk, 2, 2, 16])
        self.E(eng_a, "tensor_tensor", [Bsrc, Bcs], [Bt1], out=v3(t1), in0=v3(src), in1=cb, op=ALU.mult)
        self.E(eng_a, "tensor_tensor", [Bsrc, Bcs], [Bt2], out=v5(t2)[:, :, :, 0, :], in0=v5(src)[:, :, :, 1, :], in1=s5[:, :, :, 0, :], op=ALU.mult)
        self.E(eng_a, "tensor_tensor", [Bsrc, Bcs], [Bt2], out=v5(t2)[:, :, :, 1, :], in0=v5(src)[:, :, :, 0, :], in1=s5[:, :, :, 1, :], op=ALU.mult)
        self.E(eng_b, "tensor_tensor", [Bt1, Bt2], [Bdst], out=dst, in0=t1, in1=t2, op=ALU.add)

    def phaseC(self):
        f = self.f
        S = Alloc(self.arena, 90 * KBY, 207 * KBY)
        wb = S.get([8, 1536], BF16); Bwb = Buf("wbC")
        self.load_weight_bf16(wb, Bwb, self.w_in, 0, 1536, S)
        O = Alloc(self.arena, 130 * KBY, 189 * KBY)
        self.qT_ret = O.get([4, L], BF16); self.kT_ret = O.get([2, NK * 128], BF16)
        self.v_ret = O.get([NK, 512], BF16); self.rgT = O.get([4, L], BF16)
        self.BqT_ret, self.BkT_ret, self.Bv_ret, self.BrgT = Buf("qTr"), Buf("kTr"), Buf("vr"), Buf("rgT")
        T = Alloc(self.arena, 189 * KBY, 207 * KBY)
        self.V("memset", [], [self.BqT_ret], self.qT_ret, 0.0)
        qTv = self.qT_ret.rearrange("p (c j) t -> p c j t", j=2)
        t1 = [T.get([512], F32) for _ in range(2)]; t2 = [T.get([512], F32) for _ in range(2)]
        qkr = [T.get([512], BF16) for _ in range(2)]
        cs = [T.get([64], F32) for _ in range(2)]; sn = [T.get([64], F32) for _ in range(2)]
        Bt1 = [Buf(), Buf()]; Bt2 = [Buf(), Buf()]; Bqkr = [Buf(), Buf()]; Bcs = [Buf(), Buf()]
        Bp = [Buf() for _ in range(8)]
        def c_stage1(tt):
            s = tt % 2
            lat = tt >= NTC
            hs = self.hT[:, :, tt * 128:(tt + 1) * 128]
            pq = self.pbank(0 + s)
            for dc in range(8):
                self.MM([self.BhT[tt], Bwb], [Bp[0 + s]], pq, hs[:, dc, :], wb[:, dc, 0:512], dc == 0, dc == 7, inc=(dc == 7))
            pv = self.pbank(4 + s)
            for dc in range(8):
                self.MM([self.BhT[tt], Bwb], [Bp[4 + s]], pv, hs[:, dc, :], wb[:, dc, 512:1024], dc == 0, dc == 7, inc=(dc == 7))
            self.A([Bp[4 + s]], [self.Bv_ret], self.v_ret[:, tt, :], pv, AF.Copy)
            if lat:
                l0 = (tt - NTC) * 128
                self.dma(cs[s], self.rcos[l0:l0 + 128, :], writes=[Bcs[s]])
                self.dma(sn[s], self.rsin[l0:l0 + 128, :], writes=[Bcs[s]])
                self.rope("vector", "gpsimd", pq, Bp[0 + s], qkr[s], Bqkr[s], t1[s], t2[s], Bt1[s], Bt2[s], cs[s], sn[s], Bcs[s], 8)
            else:
                self.A([Bp[0 + s]], [Bqkr[s]], qkr[s], pq, AF.Copy)

        def c_stage2(tt):
            s = tt % 2
            lat = tt >= NTC
            pt = self.pbank(2 + s, BF16).rearrange("p (k t) -> p k t", k=8)
            cl = (0, 1, 2, 3) if lat else (2, 3)
            for c in cl:
                self.TR([Bqkr[s], self.Bc], [Bp[2 + s]], pt[:, c, :], qkr[s][:, c * 128:(c + 1) * 128], self.ident_b, inc=(c == 3))
            if lat:
                l0 = (tt - NTC) * 128
                self.A([Bp[2 + s]], [self.BqT_ret], qTv[0:64, :, 0, l0:l0 + 128], pt[0:64, 0:2, :], AF.Copy)
                self.A([Bp[2 + s]], [self.BqT_ret], qTv[64:128, :, 1, l0:l0 + 128], pt[64:128, 0:2, :], AF.Copy)
            self.A([Bp[2 + s]], [self.BkT_ret], self.kT_ret[:, :, tt * 128:(tt + 1) * 128], pt[:, 2:4, :], AF.Copy)

        c_stage1(0)
        for tt in range(NK):
            if tt + 1 < NK:
                c_stage1(tt + 1)
            c_stage2(tt)
        cnt = 0
        for g in range(4):
            for c in range(4):
                s = cnt % 2; cnt += 1
                pg = self.pbank(6 + s)
                for dc in range(8):
                    self.MM(self.BhT + [Bwb], [Bp[6 + s]], pg, wb[:, dc, 1024 + c * 128:1024 + (c + 1) * 128],
                            self.hT[:, dc, LC + g * 512:LC + (g + 1) * 512], dc == 0, dc == 7, inc=(dc == 7))
                self.A([Bp[6 + s]], [self.BrgT], self.rgT[:, c, g * 512:(g + 1) * 512], pg, AF.Silu)
        self.tap("qT_ret", self.qT_ret, self.BqT_ret)
        self.tap("kT_ret", self.kT_ret, self.BkT_ret)
        self.tap("v_ret", self.v_ret, self.Bv_ret)
        self.tap("rgT", self.rgT, self.BrgT)
        f.fence()

    def headnorm(self, y, By, sq, Bsq, pn, Bpn, rs, Brs, tmp, Btmp):
        self.A([By], [Bsq], sq, y, AF.Square)
        self.MM([Bsq, self.Bc], [Bpn], pn, self.ones_f, sq, True, True)
        self.rsqrt_chain(pn, rs, tmp, 1.0 / 128, Bpn, Brs, Btmp)

    def phaseD(self):
        f = self.f
        self.catT = Alloc(self.arena, 22 * KBY, 54 * KBY).get([8, L], BF16)
        self.BcatT = Buf("catT")
        S = Alloc(self.arena, 90 * KBY, 130 * KBY)
        W = 3968
        OFF = 1920
        Mstrip = S.get([W], BF16); Mc = S.get([2, L], BF16)
        T32 = S.get([L], F32)
        T = Alloc(self.arena, 189 * KBY, 207 * KBY)
        tmpb = T.get([L], F32)
        Y = Alloc(self.arena, S.off, 130 * KBY)
        NS = 5
        SBK = (0, 1, 2, 6, 7)
        PT = [Y.get([512], BF16) for _ in range(NS)]
        Ssb = [Y.get([512], BF16) for _ in range(NS)]
        BT32 = Buf("T32"); BSsb = [Buf() for _ in range(NS)]
        ysb = T.get([512], F32); sq = T.get([512], F32); rs = T.get([512], F32); tm = T.get([512], F32)
        BM, BMc, Btmpb = Buf("Mstrip"), Buf("Mc"), Buf("tmpb")
        BPT = [Buf() for _ in range(NS)]
        Bps = [Buf() for _ in range(8)]
        By, Bsq, Brs, Btm = Buf(), Buf(), Buf(), Buf()
        its = [(h, g, kt) for h in range(4) for g in range(4) for kt in range(NK)]
        LA = NS - 1

        def maskgen(h):
            lgf = self.lg[:, h:h + 1]; lgb = self.lg[:, 4 + h:5 + h]; nlgb = self.nlgb[:, h:h + 1]
            HW = W // 2
            for hf in range(2):
                dl = tmpb[:, 0:HW]
                ms = T32[:, 0:HW]
                self.G("iota", [], [Btmpb], dl, pattern=[[1, HW]], base=-OFF + hf * HW, channel_multiplier=-1, allow_small_or_imprecise_dtypes=True)
                self.V("tensor_scalar", [Btmpb, self.Bsmall], [BT32], out=ms, in0=dl, scalar1=0.0, scalar2=lgf, op0=ALU.max, op1=ALU.mult)
                self.V("tensor_scalar", [Btmpb, self.Bsmall], [Btmpb], out=dl, in0=dl, scalar1=0.0, scalar2=nlgb, op0=ALU.min, op1=ALU.mult)
                self.V("tensor_tensor", [Btmpb, BT32], [BT32], out=ms, in0=ms, in1=dl, op=ALU.add)
                self.A([BT32], [BM], Mstrip[:, hf * HW:(hf + 1) * HW], ms, AF.Exp)
            self.G("iota", [], [BT32], T32, pattern=[[1, L]], base=LC, channel_multiplier=-1, allow_small_or_imprecise_dtypes=True)
            self.G("iota", [], [Btmpb], tmpb, pattern=[[-1, L]], base=L, channel_multiplier=1, allow_small_or_imprecise_dtypes=True)
            self.A([BT32, self.Bsmall], [BT32], T32, T32, AF.Exp, scale=lgf)
            self.A([Btmpb, self.Bsmall], [Btmpb], tmpb, tmpb, AF.Exp, scale=lgb)
            self.V("tensor_tensor", [Btmpb, BT32], [BMc], out=Mc[:, 0, :], in0=T32, in1=tmpb, op=ALU.add)
            self.V("tensor_scalar", [BT32, self.Bsmall], [BT32], out=T32, in0=T32, scalar1=self.cshift[:, h:h + 1], scalar2=None, op0=ALU.mult)
            self.V("scalar_tensor_tensor", [Btmpb, BT32, self.Bsmall], [BMc], out=Mc[:, 1, :], in0=tmpb, scalar=self.cshift[:, 4 + h:5 + h], in1=T32, op0=ALU.mult, op1=ALU.add)

        def emitS(i):
            h, g, kt = its[i]
            c = h // 2
            s = i % NS
            self.MM([self.BkT_ret, self.BqT_ret], [Bps[SBK[s]]], self.pbank(SBK[s]), self.kT_ret[:, c, kt * 128:(kt + 1) * 128],
                    self.qT_ret[:, h, g * 512:(g + 1) * 512], True, True)

        def emitRest(i):
            h, g, kt = its[i]
            s = i % NS
            ps = self.pbank(SBK[s])
            ob = 3 + (h * 4 + g) % 2
            po, Bpo = self.pbank(ob), Bps[ob]
            if g == 0 and kt == 0:
                maskgen(h)
                if h == 0:
                    self.convert_tables(0, 8)
            if kt < NTC:
                mk, Bmk = Mc[:, kt, g * 512:(g + 1) * 512], BMc
            else:
                st = g * 512 - (kt - NTC) * 128 + OFF
                mk, Bmk = Mstrip[:, st:st + 512], BM
            self.A([Bps[SBK[s]]], [BSsb[s]], Ssb[s], ps, AF.Copy, scale=0.125)
            self.V("tensor_tensor", [BSsb[s], Bmk], [BPT[s]], out=PT[s], in0=Ssb[s], in1=mk, op=ALU.mult)
            self.MM([self.Bv_ret, BPT[s]], [Bpo], po, self.v_ret[:, kt, h * 128:(h + 1) * 128], PT[s], kt == 0, kt == NK - 1, inc=(kt == NK - 1))
            if kt == NK - 1:
                self.A([Bpo], [By], ysb, po, AF.Copy)
                self.A([By], [Bsq], sq, ysb, AF.Square)

                def fin(h=h, g=g):
                    self.MM([Bsq, self.Bc], [Bps[5]], self.pbank(5), self.ones_f, sq, True, True)
                    self.rsqrt_lnexp(self.pbank(5), rs, tm, 1.0 / 128, Bps[5], Brs, Btm)
                    self.V("tensor_tensor", [By, Brs], [Btm], out=tm, in0=ysb, in1=rs, op=ALU.mult)
                    self.V("scalar_tensor_tensor", [Btm, self.Bsmall, self.BrgT], [self.BcatT], out=self.catT[:, h, g * 512:(g + 1) * 512],
                           in0=tm, scalar=self.rg_col[:, h:h + 1], in1=self.rgT[:, h, g * 512:(g + 1) * 512], op0=ALU.mult, op1=ALU.mult)
                pending.append((i + DEFER, fin))

        pending = []
        DEFER = 8
        for i in range(len(its) + LA):
            if i < len(its):
                emitS(i)
            if i - LA >= 0:
                emitRest(i - LA)
                while pending and pending[0][0] <= i - LA:
                    pending.pop(0)[1]()
        for _, fn in pending:
            fn()
        if self.stop_after == "D":
            self.tap("catT", self.catT, self.BcatT)
        f.fence(skip=self.cv_clocks)

    def phaseE(self):
        f = self.f
        S = Alloc(self.arena, 90 * KBY, 130 * KBY)
        wb = S.get([8, 1536], BF16); Bwb = Buf("wbE")
        self.load_weight_bf16(wb, Bwb, self.w_in, 1536, 1536, S, chunk=64)
        O = Alloc(self.arena, 130 * KBY, 198 * KBY)
        self.qT_diff = O.get([8, L], BF16); self.kT_diff = O.get([4, NK * 128], BF16)
        self.v_diff = O.get([NK, 512], BF16)
        self.BqT_diff, self.BkT_diff, self.Bv_diff = Buf("qTd"), Buf("kTd"), Buf("vd")
        T = Alloc(self.arena, 198 * KBY, 207 * KBY)
        X = Alloc(self.arena, 38 * KBY, 54 * KBY)
        qk = [S.get([1024], F32), X.get([1024], F32)]; xa = [S.get([1024], F32), X.get([1024], F32)]
        xb = [S.get([1024], F32), X.get([1024], F32)]
        qkr = [T.get([1024], BF16) for _ in range(2)]
        cs = [T.get([64], F32) for _ in range(2)]; sn = [T.get([64], F32) for _ in range(2)]
        ss = [T.get([16], F32) for _ in range(2)]; vv = [T.get([16], F32) for _ in range(2)]; rstd = [T.get([16], F32) for _ in range(2)]
        Bqk, Bxa, Bxb = [Buf(), Buf()], [Buf(), Buf()], [Buf(), Buf()]
        Bqkr = [Buf(), Buf()]; Bcs = [Buf(), Buf()]; Bss = [Buf(), Buf()]
        self.V("memset", [], [self.BqT_diff], self.qT_diff, 0.0)
        qTv = self.qT_diff.rearrange("p (h j) t -> p h j t", j=2)
        Bp = [Buf() for _ in range(8)]
        v16 = lambda a: a.rearrange("p (n d) -> p n d", n=16)
        def stage1(tt):
            s = tt % 2
            hs = self.hT[:, :, tt * 128:(tt + 1) * 128]
            for half in range(2):
                b = (0, 1)[half] if s == 0 else (6, 7)[half]
                pq = self.pbank(b)
                for dc in range(8):
                    self.MM([self.BhT[tt], Bwb], [Bp[b]], pq, hs[:, dc, :], wb[:, dc, half * 512:(half + 1) * 512], dc == 0, dc == 7, inc=(dc == 7))
                self.A([Bp[b]], [Bqk[s]], qk[s][:, half * 512:(half + 1) * 512], pq, AF.Copy)
            pv = self.pbank(4 + s)
            for dc in range(8):
                self.MM([self.BhT[tt], Bwb], [Bp[4 + s]], pv, hs[:, dc, :], wb[:, dc, 1024:1536], dc == 0, dc == 7, inc=(dc == 7))
            self.A([Bp[4 + s]], [self.Bv_diff], self.v_diff[:, tt, :], pv, AF.Copy)
            self.V("tensor_tensor", [Bqk[s]], [Bxa[s]], out=xa[s], in0=qk[s], in1=qk[s], op=ALU.mult)
            self.V("tensor_reduce", [Bxa[s]], [Bss[s]], out=ss[s], in_=v16(xa[s]), axis=AX.X, op=ALU.add)
            self.rsqrt_chain(ss[s], rstd[s], vv[s], 1.0 / 64, Bss[s], Bss[s], Bss[s])
            self.V("tensor_tensor", [Bqk[s], Bss[s]], [Bxa[s]], out=v16(xa[s]), in0=v16(qk[s]), in1=rstd[s].unsqueeze(2).to_broadcast([128, 16, 64]), op=ALU.mult)

        def stage2(tt):
            s = tt % 2
            lat = tt >= NTC
            if lat:
                l0 = (tt - NTC) * 128
                self.dma(cs[s], self.rcos[l0:l0 + 128, :], writes=[Bcs[s]])
                self.dma(sn[s], self.rsin[l0:l0 + 128, :], writes=[Bcs[s]])
                self.G("tensor_tensor", [Bxa[s], self.Bsmall], [Bxb[s]], out=xb[s], in0=xa[s], in1=self.gqk, op=ALU.mult)
                self.rope("vector", "gpsimd", xb[s], Bxb[s], qkr[s], Bqkr[s], qk[s], xa[s], Bqk[s], Bxa[s], cs[s], sn[s], Bcs[s], 16)
            else:
                self.G("tensor_tensor", [Bxa[s], self.Bsmall], [Bqkr[s]], out=qkr[s], in0=xa[s], in1=self.gqk, op=ALU.mult)
            ptq = self.pbank(2, BF16).rearrange("p (k t) -> p k t", k=8)
            ptk = self.pbank(3, BF16).rearrange("p (k t) -> p k t", k=8)
            if lat:
                for c in range(4):
                    self.TR([Bqkr[s], self.Bc], [Bp[2]], ptq[:, c, :], qkr[s][:, c * 128:(c + 1) * 128], self.ident_b, inc=(c == 3))
                self.A([Bp[2]], [self.BqT_diff], qTv[0:64, :, 0, l0:l0 + 128], ptq[0:64, 0:4, :], AF.Copy)
                self.A([Bp[2]], [self.BqT_diff], qTv[64:128, :, 1, l0:l0 + 128], ptq[64:128, 0:4, :], AF.Copy)
            for c in range(4):
                self.TR([Bqkr[s], self.Bc], [Bp[3]], ptk[:, c, :], qkr[s][:, 512 + c * 128:512 + (c + 1) * 128], self.ident_b, inc=(c == 3))
            self.A([Bp[3]], [self.BkT_diff], self.kT_diff[:, :, tt * 128:(tt + 1) * 128], ptk[:, 0:4, :], AF.Copy)

        stage1(0)
        for tt in range(NK):
            if tt + 1 < NK:
                stage1(tt + 1)
            stage2(tt)
        self.tap("qT_diff", self.qT_diff, self.BqT_diff)
        self.tap("kT_diff", self.kT_diff, self.BkT_diff)
        self.tap("v_diff", self.v_diff, self.Bv_diff)
        f.fence()

    def phaseF(self):
        f = self.f
        self.convert_tables(8, 16)
        S = Alloc(self.arena, 90 * KBY, 130 * KBY)
        ET = [S.get([512], BF16) for _ in range(3)]
        rz = [S.get([512], F32) for _ in range(2)]
        ts = [S.get([512], F32) for _ in range(2)]
        ysb = S.get([512], F32); sq = S.get([512], F32); rs = S.get([512], F32); tm = S.get([512], F32)
        BET = [Buf() for _ in range(3)]
        Brz = [Buf(), Buf()]; Bts = [Buf(), Buf()]
        By, Bsq, Brs, Btm = Buf(), Buf(), Buf(), Buf()
        Bps = [Buf() for _ in range(8)]
        its = [(h, g, sub, kt) for h in range(4) for g in range(4) for sub in range(2) for kt in range(NK)]
        LA = 2

        def emitS(i):
            h, g, sub, kt = its[i]
            pb = sub * 64
            s = i % 3
            self.MM([self.BkT_diff, self.BqT_diff], [Bps[s]], self.pbank(s), self.kT_diff[:, h, kt * 128:(kt + 1) * 128],
                    self.qT_diff[:, h * 2 + sub, g * 512:(g + 1) * 512], True, True)

        def emitRest(i):
            h, g, sub, kt = its[i]
            s = i % 3
            ps = self.pbank(s)
            po = self.pbank(3 + sub); Bpo = Bps[3 + sub]
            pz = self.pbank(5 + sub); Bpz = Bps[5 + sub]
            self.A([Bps[s]], [BET[s]], ET[s], ps, AF.Exp, scale=0.125)
            last = kt == NK - 1
            self.MM([self.Bv_diff, BET[s]], [Bpo], po, self.v_diff[:, kt, h * 128:(h + 1) * 128], ET[s], kt == 0, last, inc=False)
            self.MM([self.Bc, BET[s]], [Bpz], pz, self.ones_b, ET[s], kt == 0, last, inc=last)
            if last:
                self.V("reciprocal", [Bpz], [Brz[sub]], out=rz[sub], in_=pz)
                self.V("tensor_tensor", [Bpo, Brz[sub]], [Bts[sub]], out=ts[sub], in0=po, in1=rz[sub], op=ALU.mult)
            if last and sub == 1:
                self.V("scalar_tensor_tensor", [Bts[0], Bts[1], self.Bsmall], [By], out=ysb, in0=ts[1], scalar=self.nlam[:, 0:1], in1=ts[0], op0=ALU.mult, op1=ALU.add)

                def fin(h=h, g=g):
                    self.A([By], [Bsq], sq, ysb, AF.Square)
                    self.MM([Bsq, self.Bc], [Bps[7]], self.pbank(7), self.ones_f, sq, True, True)
                    self.rsqrt_lnexp(self.pbank(7), rs, tm, 1.0 / 128, Bps[7], Brs, Btm)
                    self.V("tensor_tensor", [By, Brs], [Btm], out=tm, in0=ysb, in1=rs, op=ALU.mult)
                    self.V("tensor_scalar", [Btm, self.Bsmall], [self.BcatT], out=self.catT[:, 4 + h, g * 512:(g + 1) * 512],
                           in0=tm, scalar1=self.dg_col[:, h:h + 1], scalar2=None, op0=ALU.mult)
                pending.append((i + DEFER, fin))

        pending = []
        DEFER = 10
        for i in range(len(its) + LA):
            if i < len(its):
                emitS(i)
            if i - LA >= 0:
                emitRest(i - LA)
                while pending and pending[0][0] <= i - LA:
                    pending.pop(0)[1]()
        for _, fn in pending:
            fn()
        self.tap("catT", self.catT, self.BcatT)
        f.fence(skip=self.cv_clocks)


    def convert_tables(self, lo, hi):
        R = 2048
        jobs = [(c, tab, off) for c in range(NEXP // R) for tab, off in ((self.pu, 0), (self.pv, D))]
        for n in range(lo, hi):
            c, tab, off = jobs[n]
            clk = self.f.dma_clock(f"cv{n}")
            self.cv_clocks.append(clk)
            self.dma(self.puv[c * R:(c + 1) * R, off:off + D], tab[c * R:(c + 1) * R, :], writes=[self.Bpuv], eng="gpsimd", clock=clk)

    def phaseG(self):
        f = self.f
        self.x1 = Alloc(self.arena, 54 * KBY, 118 * KBY).get([NT, D], F32)
        self.Bx1 = [Buf(f"x1_{t}") for t in range(NT)]
        S = Alloc(self.arena, 118 * KBY, 207 * KBY)
        wo = S.get([8, D], BF16); Bwo = Buf("wo")
        self.load_weight_bf16(wo, Bwo, self.w_out, 0, D, S, cast_engs=("vector", "scalar"))
        xin = [S.get([D], F32) for _ in range(2)]; tmp = [S.get([D], F32) for _ in range(2)]
        Bx = [Buf(), Buf()]; Bt = [Buf(), Buf()]
        Bp = [Buf() for _ in range(8)]
        gate1 = self.modp[:, 0:D]
        for tt in range(NT):
            s = tt % 2
            self.dma(xin[s], self.x[tt * 128:(tt + 1) * 128, :], writes=[Bx[s]])
            for half in range(2):
                b = 2 * s + half
                pb = self.pbank(b)
                for kc in range(8):
                    self.MM([self.BcatT, Bwo], [Bp[b]], pb, self.catT[:, kc, tt * 128:(tt + 1) * 128], wo[:, kc, half * 512:(half + 1) * 512], kc == 0, kc == 7, inc=(kc == 7))
                self.V("tensor_tensor", [Bp[b], self.Bmodp], [Bt[s]], out=tmp[s][:, half * 512:(half + 1) * 512], in0=pb, in1=gate1[:, half * 512:(half + 1) * 512], op=ALU.mult)
            self.V("tensor_tensor", [Bt[s], Bx[s]], [self.Bx1[tt]], out=self.x1[:, tt, :], in0=tmp[s], in1=xin[s], op=ALU.add)
        self.tap("x1", self.x1, self.Bx1[NT - 1])
        f.fence(skip=getattr(self, "cv_clocks", ()))

    def phaseH(self):
        f = self.f
        NB = 8
        GK = 2
        wqb = Alloc(self.arena, 22 * KBY, 54 * KBY).get([8, 2048], BF16); Bwq = Buf("wq")
        S = Alloc(self.arena, 118 * KBY, 207 * KBY)
        skT = S.get([16, 128], BF16); BskT = Buf("skT")
        h2b = [S.get([D], BF16) for _ in range(2)]; h2T = S.get([8, 128], BF16)
        tmp = S.get([D], F32); qTs = S.get([16, 128], BF16); junk = S.get([D], BF16)
        SA = S.get([2048], F32); SB = S.get([2048], F32)
        v16 = S.get([16, 16], F32); ix = S.get([16, 16], U32); ixf = S.get([16, 16], F32)
        best = S.get([8, 16], F32); pos = S.get([8, 16], U32); pa = S.get([128], U32); pbb = S.get([128], U32)
        paf = S.get([8, 16], F32); pbf = S.get([8, 16], F32); i1s = S.get([8, 16], F32); i2s = S.get([8, 16], F32)
        eif = S.get([128], F32); eidx = [S.get([128], I32) for _ in range(2)]
        gexp = S.get([8, 16], F32); gsum = S.get([8], F32); gate = [S.get([128], F32) for _ in range(2)]
        dots = [S.get([128], F32) for _ in range(2)]; wgt = [S.get([128], F32) for _ in range(2)]
        iotaK = S.get([16, 16], F32); ss = S.get([NT], F32); vv = S.get([NT], F32); rstd = S.get([NT], F32)
        ring = [S.get([2 * D], BF16) for _ in range(NB)]
        vb = [S.get([D], BF16) for _ in range(2)]
        dgw = [vb[0][:, 0:128], vb[0][:, 128:256], vb[0][:, 256:384], vb[0][:, 384:512]]
        Bdg = [Buf() for _ in range(4)]
        Bh2 = [Buf(), Buf()]; Bh2T, Btmp, BqTs, BSA, BSB, Bjunk = Buf(), Buf(), Buf(), Buf(), Buf(), Buf()
        Brt = Buf("route"); Beidx = [Buf(), Buf()]; Bgate = [Buf(), Buf()]
        Bdots = [[Buf() for _ in range(128 // GK)] for _ in range(2)]; Bw = [[Buf() for _ in range(128 // GK)] for _ in range(2)]
        Bio, Bss = Buf(), Buf()
        Brg = [Buf() for _ in range(NB)]; Bvb = [Buf(), Buf()]
        gclk = [f.dma_clock(f"pg{i}") for i in range(NB)]
        Bp = [Buf() for _ in range(8)]
        stgv = [ring[i].bitcast(F32).rearrange("p (a b) -> p a b", a=8) for i in range(2)]
        self.load_weight_bf16(wqb, Bwq, self.wq, 0, 2048, S, chunk=128, stg=stgv, Bst=[Brg[0], Brg[1]], cast_engs=("vector", "scalar"))
        S2 = self.modp[:, D:2 * D]; A2 = self.modp[:, 2 * D:3 * D]; gate2 = self.modp[:, 3 * D:4 * D]
        skst = [ring[2 + i].bitcast(F32).rearrange("p (a b) -> p a b", a=8) for i in range(2)]
        skb = vb[0].rearrange("p (a b) -> p a b", a=8)
        Bsk = Buf()
        for half in range(2):
            self.dma(skst[half], self.sk[half * 8:(half + 1) * 8].rearrange("h n c -> n h c"), writes=[Brg[2 + half]])
            self.V("tensor_copy", [Brg[2 + half]], [Bsk], out=skb, in_=skst[half])
            pt = self.pbank(0, BF16).rearrange("p (k t) -> p k t", k=8)
            for k in range(8):
                self.TR([Bsk, self.Bc], [Bp[0]], pt[:, k, :], skb[:, k, :], self.ident_b, inc=(k == 7))
            self.A([Bp[0]], [BskT], skT[:, half * 8:(half + 1) * 8, :], pt, AF.Copy)
        for a in range(16):
            self.V("memset", [], [Bio], iotaK[:, :, a:a + 1], float(a))
        self.V("memset", [], [Bss], ss, 0.0)
        from collections import deque
        free = deque(range(NB))

        def add_slot(j):
            ring.append(self.x1[:, j, :].bitcast(BF16))
            Brg.append(self.Bx1[j])
            gclk.append(f.dma_clock(f"pgx{j}"))
            free.appendleft(len(ring) - 1)

        def route(tt):
            s = tt % 2
            xt = self.x1[:, tt, :]
            self.A([self.Bx1[tt]], [Btmp, Bss], tmp, xt, AF.Square, accum_out=ss[:, tt:tt + 1])
            self.rsqrt_lnexp(ss[:, tt:tt + 1], rstd[:, tt:tt + 1], vv[:, tt:tt + 1], 1.0 / D, Bss, Bss, Bss)
            self.V("scalar_tensor_tensor", [self.Bx1[tt], Bss, self.Bmodp], [Btmp], out=tmp, in0=xt, scalar=rstd[:, tt:tt + 1], in1=A2, op0=ALU.mult, op1=ALU.mult)
            yield
            self.V("tensor_tensor", [Btmp, self.Bmodp], [Bh2[s]], out=h2b[s], in0=tmp, in1=S2, op=ALU.add)
            pt = self.pbank(0, BF16).rearrange("p (k t) -> p k t", k=8)
            for dc in range(8):
                self.TR([Bh2[s], self.Bc], [Bp[0]], pt[:, dc, :], h2b[s][:, dc * 128:(dc + 1) * 128], self.ident_b, inc=(dc == 7))
            self.A([Bp[0]], [Bh2T], h2T, pt, AF.Copy)
            yield
            for half in range(2):
                for q4 in range(2):
                    b = 1 + q4
                    pq = self.pbank(b).rearrange("p (k t) -> p k t", k=4)
                    for j in range(4):
                        hp = half * 8 + q4 * 4 + j
                        for dc in range(8):
                            self.MM([Bh2T, Bwq], [Bp[b]], pq[:, j, :], wqb[:, dc, hp * 128:(hp + 1) * 128], h2T[:, dc, :], dc == 0, dc == 7, inc=(dc == 7 and j == 3))
                    self.A([Bp[b]], [BqTs], qTs[:, half * 8 + q4 * 4:half * 8 + q4 * 4 + 4, :], pq, AF.Copy)
                    yield
            for q4 in range(4):
                b = 3 + q4 % 2
                psx = self.pbank(b).rearrange("p (k t) -> p k t", k=4)
                for j in range(4):
                    hp = q4 * 4 + j
                    self.MM([BqTs, BskT], [Bp[b]], psx[:, j, :], qTs[:, hp, :], skT[:, hp, :], True, True, inc=(j == 3))
                self.A([Bp[b]], [BSA], SA[:, q4 * 512:(q4 + 1) * 512], self.pbank(b), AF.Copy)
            yield
            SAv = SA.rearrange("p (g n) -> p g n", g=16); SBv = SB.rearrange("p (g n) -> p g n", g=16)
            for g in range(16):
                self.V("max", [BSA], [Brt], out=v16[:, g, 0:8], in_=SAv[:, g, :])
                self.V("max_index", [BSA, Brt], [Brt], out=ix[:, g, 0:8], in_max=v16[:, g, 0:8], in_values=SAv[:, g, :])
                self.V("match_replace", [BSA, Brt], [BSB], out=SBv[:, g, :], in_to_replace=v16[:, g, 0:8], in_values=SAv[:, g, :], imm_value=-1e30)
                yield
                self.V("max", [BSB], [Brt], out=v16[:, g, 8:16], in_=SBv[:, g, :])
                self.V("max_index", [BSB, Brt], [Brt], out=ix[:, g, 8:16], in_max=v16[:, g, 8:16], in_values=SBv[:, g, :])
                yield
            self.V("tensor_copy", [Brt], [Brt], out=ixf, in_=ix)
            vv4 = v16.rearrange("p (h t) k -> p h t k", t=2)
            cand = SA.rearrange("p (h a b) -> p h a b", h=8, a=16)
            self.V("tensor_tensor", [Brt, BSA], [BSA], out=cand, in0=vv4[:, :, 0, :].unsqueeze(3).to_broadcast([128, 8, 16, 16]),
                   in1=vv4[:, :, 1, :].unsqueeze(2).to_broadcast([128, 8, 16, 16]), op=ALU.add)
            yield
            c3 = SA.rearrange("p (h n) -> p h n", h=8); c3b = SB.rearrange("p (h n) -> p h n", h=8)
            for h in range(8):
                self.V("max", [BSA], [Brt], out=best[:, h, 0:8], in_=c3[:, h, :])
                self.V("max_index", [BSA, Brt], [Brt], out=pos[:, h, 0:8], in_max=best[:, h, 0:8], in_values=c3[:, h, :])
                self.V("match_replace", [BSA, Brt], [BSB], out=c3b[:, h, :], in_to_replace=best[:, h, 0:8], in_values=c3[:, h, :], imm_value=-1e30)
                yield
                self.V("max", [BSB], [Brt], out=best[:, h, 8:16], in_=c3b[:, h, :])
                self.V("max_index", [BSB, Brt], [Brt], out=pos[:, h, 8:16], in_max=best[:, h, 8:16], in_values=c3b[:, h, :])
                yield
            posf = pos.rearrange("p h k -> p (h k)")
            self.V("tensor_single_scalar", [Brt], [Brt], out=pa, in_=posf, scalar=4, op=ALU.logical_shift_right)
            self.V("tensor_single_scalar", [Brt], [Brt], out=pbb, in_=posf, scalar=15, op=ALU.bitwise_and)
            self.V("tensor_copy", [Brt], [Brt], out=paf.rearrange("p h k -> p (h k)"), in_=pa)
            self.V("tensor_copy", [Brt], [Brt], out=pbf.rearrange("p h k -> p (h k)"), in_=pbb)
            yield
            EQ = SB.rearrange("p (h k a) -> p h k a", h=8, k=16)
            ix4 = ixf.rearrange("p (h t) k -> p h t k", t=2)
            iob = iotaK.unsqueeze(1).to_broadcast([128, 8, 16, 16])
            for which, (pf, dst) in enumerate(((paf, i1s), (pbf, i2s))):
                self.V("tensor_tensor", [Brt, Bio, BSB], [BSB], out=EQ, in0=iob, in1=pf.unsqueeze(3).to_broadcast([128, 8, 16, 16]), op=ALU.is_equal)
                yield
                self.V("tensor_tensor", [Brt, BSB], [BSB], out=EQ, in0=EQ, in1=ix4[:, :, which, :].unsqueeze(2).to_broadcast([128, 8, 16, 16]), op=ALU.mult)
                yield
                self.V("tensor_reduce", [BSB], [Brt], out=dst, in_=EQ, axis=AX.X, op=ALU.add)
                yield
            self.V("scalar_tensor_tensor", [Brt], [Brt], out=eif, in0=i1s.rearrange("p h k -> p (h k)"), scalar=128.0, in1=i2s.rearrange("p h k -> p (h k)"), op0=ALU.mult, op1=ALU.add)
            self.V("tensor_copy", [Brt], [Beidx[s]], out=eidx[s], in_=eif)
            self.V("tensor_tensor", [Brt], [Brt], out=gexp, in0=best, in1=best[:, :, 0:1].to_broadcast([128, 8, 16]), op=ALU.subtract)
            self.A([Brt], [Brt], gexp, gexp, AF.Exp)
            self.V("tensor_reduce", [Brt], [Brt], out=gsum, in_=gexp, axis=AX.X, op=ALU.add)
            self.V("reciprocal", [Brt], [Brt], out=gsum, in_=gsum)
            self.V("tensor_tensor", [Brt], [Bgate[s]], out=gate[s].rearrange("p (h k) -> p h k", h=8), in0=gexp, in1=gsum.unsqueeze(2).to_broadcast([128, 8, 16]), op=ALU.mult)
            self.V("memset", [], Bdots[s], dots[s], 0.0)
            if tt == 0 and not getattr(self.f, "dry", False):
                self.tap("eidx0", eidx[0], Beidx[0])
                self.tap("gate0", gate[0], Bgate[0])

        def experts(tt, bg):
            s = tt % 2
            ngrp = 128 // GK
            slots = {}

            def issue(grp):
                for j in range(GK):
                    hk = grp * GK + j
                    r = free.popleft()
                    free.append(r)
                    slots[hk] = r
                    self.f.op("gpsimd", (lambda r=r, hk=hk, s=s: lambda e: e.indirect_dma_start(
                        out=ring[r], out_offset=None, in_=self.puv,
                        in_offset=bass.IndirectOffsetOnAxis(ap=eidx[s][:, hk:hk + 1], axis=0)))(),
                        reads=[Beidx[s], self.Bpuv], writes=[Brg[r]], clock=gclk[r])

            def dots_of(grp):
                for j in range(GK):
                    hk = grp * GK + j
                    r = slots[hk]
                    self.V("scalar_tensor_tensor", [Brg[r], Bh2[s]], [Brg[r], Bdots[s][grp]], out=ring[r][:, 0:D], in0=ring[r][:, 0:D], scalar=1.0, in1=h2b[s],
                           op0=ALU.mult, op1=ALU.mult, accum_out=dots[s][:, hk:hk + 1])

            def finish_of(grp):
                sl = slice(grp * GK, (grp + 1) * GK)
                self.A([Bdots[s][grp]], [Bw[s][grp]], wgt[s][:, sl], dots[s][:, sl], AF.Gelu)
                for j in range(GK):
                    hk = grp * GK + j
                    r = slots[hk]
                    k = hk % 4
                    self.A([Bw[s][grp], Bgate[s]], [Bw[s][grp]], wgt[s][:, hk:hk + 1], wgt[s][:, hk:hk + 1], AF.Identity, scale=gate[s][:, hk:hk + 1])
                    self.A([self.Bc, Bw[s][grp]], [Bdg[k]], dgw[k], self.ident_b, AF.Identity, scale=wgt[s][:, hk:hk + 1])
                    for half in range(2):
                        self.MM([Bdg[k], Brg[r]], [Bp[5 + half]], self.pbank(5 + half), dgw[k], ring[r][:, D + half * 512:D + (half + 1) * 512],
                                hk == 0, hk == 127, inc=(half == 1))

            LAG = min(8, len(ring) // GK - 1)
            assert LAG >= 1 and (LAG + 1) * GK <= len(ring)
            for g0 in range(LAG):
                issue(g0)
            for grp in range(ngrp):
                if grp + LAG < ngrp:
                    issue(grp + LAG)
                dots_of(grp)
                finish_of(grp)
                if bg is not None:
                    for _ in range(((grp + 1) * nyield + ngrp - 9) // (ngrp - 8) - (grp * nyield + ngrp - 9) // (ngrp - 8)):
                        next(bg, None)
            if tt == 0:
                self.tap("w0", wgt[0], Bw[0][128 // GK - 1])
            if bg is not None:
                for _ in bg:
                    pass
            for half in range(2):
                self.V("tensor_tensor", [Bp[5 + half], self.Bmodp], [Btmp], out=tmp[:, half * 512:(half + 1) * 512], in0=self.pbank(5 + half),
                       in1=gate2[:, half * 512:(half + 1) * 512], op=ALU.mult)
            self.V("tensor_tensor", [Btmp, self.Bx1[tt]], [self.Bx1[tt]], out=self.x1[:, tt, :], in0=tmp, in1=self.x1[:, tt, :], op=ALU.add)
            self.dma(self.out[tt * 128:(tt + 1) * 128, :], self.x1[:, tt, :], reads=[self.Bx1[tt]], writes=[self.Bout], clock=self.stc)

        ntiles = NT if self.stop_after != "H1" else 1
        self.f.dry = True
        nyield = sum(1 for _ in route(1)) + 1
        self.f.dry = False
        for _ in route(0):
            pass
        for tt in range(ntiles):
            experts(tt, route(tt + 1) if tt + 1 < ntiles else None)
            if tt + 2 < ntiles:
                add_slot(tt)


def rope_tables():
    quarter = 16
    freqs = (10000.0 ** (-np.arange(quarter, dtype=np.float32) / quarter)).astype(np.float32)
    rows = L // 64
    row = np.repeat(np.arange(rows, dtype=np.float32), 64)
    col = np.tile(np.arange(64, dtype=np.float32), rows)
    ar = row[:, None] * freqs
    ac = col[:, None] * freqs
    ang = np.concatenate([ar, ar, ac, ac], axis=-1).astype(np.float32)
    cos = np.cos(ang).astype(np.float32)
    sin = np.sin(ang).astype(np.float32)
    sgn = np.concatenate([-np.ones(16), np.ones(16), -np.ones(16), np.ones(16)]).astype(np.float32)
    return cos, (sin * sgn).astype(np.float32)


def make_in_map(inp, b):
    f = lambda a: np.ascontiguousarray(np.asarray(a, dtype=np.float32))
    cos, sins = rope_tables()
    return {
        "x": f(inp["x"][b]), "ctx": f(inp["ctx"][b]), "c": f(inp["c"][b]), "c_ctx": f(inp["c_ctx"]),
        "w_mod": f(inp["w_mod"][0]), "b_mod": f(inp["b_mod"][0]),
        "norm1_g": f(inp["norm1_g"][0]), "norm2_g": f(inp["norm2_g"][0]),
        "w_in": f(inp["w_in"][0]), "decay": f(inp["ret_decay_logit"][0]).reshape(8),
        "ret_norm_g": f(inp["ret_norm_g"][0]), "qk_g": f(inp["diff_qk_norm_g"][0]).reshape(128),
        "dlam": f(inp["diff_lambda"][0]).reshape(256), "diff_norm_g": f(inp["diff_norm_g"][0]),
        "w_out": f(inp["w_out"][0]), "wq": f(inp["peer_w_query"][0]),
        "sk": f(inp["peer_sub_keys"][0]).reshape(16, 128, 128),
        "pu": f(inp["peer_u"][0]), "pv": f(inp["peer_v"][0]),
        "rcos": cos, "rsin": sins,
    }


def kernel(**inputs):
    nc = Builder().build()
    in_maps = [make_in_map(inputs, b) for b in range(8)]
    res = run_bass_kernel_spmd(nc, in_maps, core_ids=list(range(8)))
    return np.stack([np.asarray(r["out"], dtype=np.float32) for r in res.results], axis=0)
```

```python
import math
from contextlib import ExitStack

import numpy as np
import concourse.bass as bass
import concourse.mybir as mybir
from concourse.bass_utils import run_bass_kernel_spmd

F32 = mybir.dt.float32
BF16 = mybir.dt.bfloat16
I32 = mybir.dt.int32
U32 = mybir.dt.uint32
U8 = mybir.dt.uint8
AF = mybir.ActivationFunctionType
ALU = mybir.AluOpType
AX = mybir.AxisListType
DTB = {F32: 4, BF16: 2, I32: 4, U32: 4, U8: 1}

D = 1024
L = 2048
LC = 256
NT = L // 128
NTC = LC // 128
NK = NT + NTC
EPS = 1e-6
LAM_INIT = 0.8 - 0.6 * math.exp(-0.3 * 0)
NEXP = 16384


class Clock:
    def __init__(self, name, step):
        self.name, self.step, self.count, self.sem = name, step, 0, None


class Buf:
    def __init__(self, name=""):
        self.name = name
        self.w = None
        self.r = {}


class FW:
    ENGS = ("sync", "scalar", "vector", "gpsimd", "tensor")

    def __init__(self, nc):
        self.nc = nc
        self.streams = {e: [] for e in self.ENGS}
        self.eclk = {e: Clock("e_" + e, 1) for e in self.ENGS}
        self.clocks = list(self.eclk.values())
        self.seen = {e: {} for e in self.ENGS}
        self.pend_waits = {e: {} for e in self.ENGS}
        self.pend_rw = {e: ([], []) for e in self.ENGS}
        self.nops = 0

    def dma_clock(self, name):
        c = Clock("d_" + name, 16)
        self.clocks.append(c)
        return c

    def _need(self, eng, clk, val, waits):
        if self.seen[eng].get(clk, 0) >= val:
            return
        self.seen[eng][clk] = val
        waits[clk] = max(waits.get(clk, 0), val)

    def fence(self, skip=()):
        for e in self.ENGS:
            r, w = self.pend_rw[e]
            assert not r and not w, "fence with unpublished ops on " + e
            for c in self.clocks:
                if c in skip:
                    continue
                if c.count > 0 and self.seen[e].get(c, 0) < c.count:
                    self.seen[e][c] = c.count
                    self.pend_waits[e][c] = c.count

    def op(self, eng, fn, reads=(), writes=(), clock=None, inc=True):
        if getattr(self, "dry", False):
            return None
        own = self.eclk[eng]
        clk = clock if clock is not None else own
        waits = self.pend_waits[eng]
        self.pend_waits[eng] = {}
        for b in reads:
            if b.w is not None:
                c, v = b.w
                if c == "pending":
                    assert v == eng, f"read of unpublished buffer {b.name} from {eng}"
                    continue
                if c is own and eng == "tensor":
                    continue
                self._need(eng, c, v, waits)
        same_ok = (eng == "tensor")
        for b in writes:
            if b.w is not None:
                c, v = b.w
                if c == "pending":
                    assert v == eng, f"write of unpublished buffer {b.name} from {eng}"
                elif c is not own or not same_ok:
                    self._need(eng, c, v, waits)
            for c, v in b.r.items():
                if c is not own or not same_ok:
                    self._need(eng, c, v, waits)
        if clock is not None and clock.count > 0:
            self._need(eng, clock, clock.count, waits)
        if not inc:
            assert clock is None
            pr, pw = self.pend_rw[eng]
            pr.extend(reads)
            pw.extend(writes)
            for b in writes:
                b.w = ("pending", eng)
                b.r = {}
            self.streams[eng].append((list(waits.items()), fn, None))
            self.nops += 1
            return None
        clk.count += clk.step
        val = clk.count
        allr, allw = list(reads), list(writes)
        if clock is None:
            pr, pw = self.pend_rw[eng]
            allr += pr
            allw += pw
            self.pend_rw[eng] = ([], [])
        for b in allr:
            b.r[clk] = val
        for b in allw:
            b.w = (clk, val)
            b.r = {}
        self.streams[eng].append((list(waits.items()), fn, clk))
        self.nops += 1
        return val

    def emit(self):
        nc = self.nc
        for e in self.ENGS:
            r, w = self.pend_rw[e]
            assert not r and not w, "unpublished ops at end on " + e
        with ExitStack() as es:
            for c in self.clocks:
                c.sem = es.enter_context(nc.semaphore(c.name))
            block = es.enter_context(nc.Block())
            fw = self

            def run(eng_name):
                def body(eng):
                    for waits, fn, clk in fw.streams[eng_name]:
                        for c, v in waits:
                            eng.wait_ge(c.sem, v)
                        ins = fn(eng)
                        if clk is not None:
                            ins.then_inc(clk.sem, clk.step)
                    if eng_name == "sync":
                        for c in fw.clocks:
                            if c.count > 0:
                                eng.wait_ge(c.sem, c.count)
                return body

            block.sync(run("sync"))
            block.scalar(run("scalar"))
            block.vector(run("vector"))
            block.gpsimd(run("gpsimd"))
            block.tensor(run("tensor"))


class Alloc:
    def __init__(self, arena, base, limit):
        self.arena, self.off, self.limit = arena, base, limit

    def get(self, shape, dt, parts=128):
        n = int(np.prod(shape)) * DTB[dt]
        n = (n + 31) // 32 * 32
        assert self.off + n <= self.limit, f"arena overflow {self.off}+{n} > {self.limit}"
        v = self.arena[0:parts, self.off:self.off + n].bitcast(dt)
        self.off += n
        if len(shape) == 1:
            return v[:, 0:shape[0]]
        names = " ".join(f"a{i}" for i in range(len(shape)))
        kw = {f"a{i}": s for i, s in enumerate(shape)}
        v = v[:, 0:int(np.prod(shape))]
        if len(shape) > 1:
            v = v.rearrange(f"p ({names}) -> p {names}", **kw)
        return v


KBY = 1024


class Builder:
    def __init__(self, stop_after=None, taps=()):
        self.stop_after = stop_after
        self.want_taps = set(taps)
        self.taps = {}
        self.nc = nc = bass.Bass("TRN2", target_bir_lowering=False)
        self.f = FW(nc)
        self.es = ExitStack()
        di = lambda name, shape, dt=F32: nc.dram_tensor(name, list(shape), dt, kind="ExternalInput").ap()
        self.x = di("x", [L, D]); self.ctx = di("ctx", [LC, D])
        self.c = di("c", [D]); self.c_ctx = di("c_ctx", [D])
        self.w_mod = di("w_mod", [D, 6 * D]); self.b_mod = di("b_mod", [6 * D])
        self.norm1_g = di("norm1_g", [D]); self.norm2_g = di("norm2_g", [D])
        self.w_in = di("w_in", [D, 3072])
        self.decay = di("decay", [8]); self.ret_norm_g = di("ret_norm_g", [4, 128])
        self.qk_g = di("qk_g", [128]); self.dlam = di("dlam", [256]); self.diff_norm_g = di("diff_norm_g", [4, 128])
        self.w_out = di("w_out", [D, D]); self.wq = di("wq", [D, 2048]); self.sk = di("sk", [16, 128, 128])
        self.pu = di("pu", [NEXP, D]); self.pv = di("pv", [NEXP, D])
        self.rcos = di("rcos", [L, 64]); self.rsin = di("rsin", [L, 64])
        self.out = nc.dram_tensor("out", [L, D], F32, kind="ExternalOutput").ap()
        self.puv = nc.dram_tensor("puv", [NEXP, 2 * D], BF16, kind="Internal").ap()
        self.Bpuv = Buf("puv")
        self.arena = self.es.enter_context(nc.sbuf_tensor("arena", [128, 207 * KBY], U8))
        self.psum = self.es.enter_context(nc.psum_tensor("psum", [128, 8, 2048], U8))
        self.ld = [self.f.dma_clock(f"ld{i}") for i in range(16)]
        self.ldi = 0
        self.stc = self.f.dma_clock("st")
        self.Bout = Buf("out")
        self.cv_clocks = []

    def pbank(self, b, dt=F32, nb=1):
        if nb == 1:
            return self.psum[:, b, :].bitcast(dt)
        return self.psum[:, b:b + nb, :].bitcast(dt)

    def E(self, eng, method, reads, writes, *a, inc=True, clock=None, **kw):
        return self.f.op(eng, lambda e: getattr(e, method)(*a, **kw), reads=reads, writes=writes, clock=clock, inc=inc)

    def V(self, method, reads, writes, *a, **kw):
        return self.E("vector", method, reads, writes, *a, **kw)

    def G(self, method, reads, writes, *a, **kw):
        return self.E("gpsimd", method, reads, writes, *a, **kw)

    def A(self, reads, writes, out, in_, func, **kw):
        return self.E("scalar", "activation", reads, writes, out=out, in_=in_, func=func, **kw)

    def MM(self, reads, writes, out, lhsT, rhs, start, stop, inc=True):
        return self.E("tensor", "matmul", reads, writes, out, lhsT=lhsT, rhs=rhs, start=start, stop=stop, inc=inc)

    def TR(self, reads, writes, out, in_, ident, inc=True):
        return self.E("tensor", "transpose", reads, writes, out=out, in_=in_, identity=ident, inc=inc)

    def dma(self, out, in_, reads=(), writes=(), eng="sync", clock=None):
        if clock is None:
            clock = self.ld[self.ldi % len(self.ld)]
            self.ldi += 1
        return self.E(eng, "dma_start", reads, writes, out=out, in_=in_, clock=clock)

    def tap(self, name, ap, buf):
        if name not in self.want_taps:
            return
        shape = list(ap.shape)
        o = self.nc.dram_tensor("tap_" + name, shape, ap.dtype, kind="ExternalOutput").ap()
        self.taps[name] = o
        self.dma(o, ap, reads=[buf], writes=[Buf()], clock=self.stc)

    def rsqrt_chain(self, src, dst, tmp, n_inv, Bsrc, Bdst, Btmp):
        self.V("tensor_scalar", [Bsrc], [Btmp], out=tmp, in0=src, scalar1=n_inv, scalar2=EPS, op0=ALU.mult, op1=ALU.add)
        self.A([Btmp], [Btmp], tmp, tmp, AF.Sqrt)
        self.V("reciprocal", [Btmp], [Bdst], out=dst, in_=tmp)

    def rsqrt_lnexp(self, src, dst, tmp, n_inv, Bsrc, Bdst, Btmp):
        self.V("tensor_scalar", [Bsrc], [Btmp], out=tmp, in0=src, scalar1=n_inv, scalar2=EPS, op0=ALU.mult, op1=ALU.add)
        self.A([Btmp], [Btmp], tmp, tmp, AF.Ln)
        self.A([Btmp], [Bdst], dst, tmp, AF.Exp, scale=-0.5)

    def build(self):
        nc, f = self.nc, self.f
        AR = self.arena
        P = Alloc(AR, 0, 22 * KBY)
        self.ident_f = P.get([128], F32); self.ident_b = P.get([128], BF16)
        self.ones_f = P.get([128], F32); self.ones_b = P.get([128], BF16)
        self.modp = P.get([4 * D], F32)
        self.lg = P.get([8], F32)
        self.nlgb = P.get([4], F32)
        self.cshift = P.get([8], F32)
        self.rg_col = P.get([4], F32)
        self.dg_col = P.get([4], F32)
        self.nlam = P.get([1], F32)
        self.gqk = P.get([1024], F32)
        self.Bc = Buf("consts")
        self.Bmodp = Buf("modp")
        self.Bsmall = Buf("small")
        self.setup_consts(P)
        self.phaseA()
        if self.stop_after == "A":
            return self.finish()
        self.phaseB()
        if self.stop_after == "B":
            return self.finish()
        self.phaseC()
        if self.stop_after == "C":
            return self.finish()
        self.phaseD()
        if self.stop_after == "D":
            return self.finish()
        self.phaseE()
        if self.stop_after == "E":
            return self.finish()
        self.phaseF()
        if self.stop_after == "F":
            return self.finish()
        self.phaseG()
        if self.stop_after == "G":
            return self.finish()
        self.phaseH()
        return self.finish()

    def finish(self):
        self.f.emit()
        self.es.close()
        return self.nc

    def setup_consts(self, P):
        Bc = self.Bc
        self.G("memset", [], [Bc], self.ones_f, 1.0)
        self.G("memset", [], [Bc], self.ones_b, 1.0)
        self.G("affine_select", [Bc], [Bc], out=self.ident_f, in_=self.ones_f, pattern=[[-1, 128]],
               compare_op=ALU.is_equal, fill=0.0, base=0, channel_multiplier=1)
        self.V("tensor_copy", [Bc], [Bc], out=self.ident_b, in_=self.ident_f)
        Bs = self.Bsmall
        T = Alloc(self.arena, 200 * KBY, 207 * KBY)
        dl = T.get([8], F32); e1 = T.get([8], F32)
        self.dma(dl, self.decay.partition_broadcast(128), writes=[Bs])
        self.A([Bs], [Bs], e1, dl, AF.Exp, scale=-1.0)
        self.A([Bs], [Bs], e1, e1, AF.Ln, bias=1.0)
        self.V("tensor_scalar", [Bs], [Bs], out=self.lg, in0=e1, scalar1=-1.0, scalar2=None, op0=ALU.mult)
        self.V("tensor_copy", [Bs], [Bs], out=self.nlgb, in_=e1[:, 4:8])
        self.A([Bs], [Bs], self.cshift[:, 0:4], self.lg[:, 0:4], AF.Exp, scale=-128.0)
        self.A([Bs], [Bs], self.cshift[:, 4:8], self.lg[:, 4:8], AF.Exp, scale=128.0)
        for h in range(4):
            self.dma(self.rg_col[:, h:h + 1], self.ret_norm_g[h, :].rearrange("(p o) -> p o", o=1), writes=[Bs])
            self.dma(self.dg_col[:, h:h + 1], self.diff_norm_g[h, :].rearrange("(p o) -> p o", o=1), writes=[Bs])
        self.V("tensor_scalar", [Bs], [Bs], out=self.dg_col, in0=self.dg_col, scalar1=1.0 - LAM_INIT, scalar2=None, op0=ALU.mult)
        lamb = T.get([256], F32); pr = T.get([128], F32); s2 = T.get([2], F32)
        self.dma(lamb, self.dlam.partition_broadcast(128), writes=[Bs])
        lv = lamb.rearrange("p (a b d) -> p a b d", a=2, b=2)
        self.V("tensor_tensor", [Bs], [Bs], out=pr.rearrange("p (a d) -> p a d", a=2), in0=lv[:, :, 0, :], in1=lv[:, :, 1, :], op=ALU.mult)
        self.V("tensor_reduce", [Bs], [Bs], out=s2, in_=pr.rearrange("p (a d) -> p a d", a=2), axis=AX.X, op=ALU.add)
        self.A([Bs], [Bs], s2, s2, AF.Exp)
        self.V("tensor_tensor", [Bs], [Bs], out=self.nlam, in0=s2[:, 1:2], in1=s2[:, 0:1], op=ALU.subtract)
        self.V("tensor_scalar", [Bs], [Bs], out=self.nlam, in0=self.nlam, scalar1=-LAM_INIT, scalar2=None, op0=ALU.add)
        gq = T.get([128], F32)
        self.dma(gq, self.qk_g.partition_broadcast(128), writes=[Bs])
        gv = self.gqk.rearrange("p (a n d) -> p a n d", a=2, n=8)
        self.V("tensor_copy", [Bs], [Bs], out=gv, in_=gq.rearrange("p (a d) -> p a d", a=2).unsqueeze(2).to_broadcast([128, 2, 8, 64]))
        self.tap("lg", self.lg, Bs)
        self.tap("nlam", self.nlam, Bs)

    def phaseA(self):
        f = self.f
        S = Alloc(self.arena, 90 * KBY, 207 * KBY)
        self.modA = S.get([2 * D], F32)
        self.modC = S.get([2 * D], F32)
        self.BmodA, self.BmodC = Buf("modA"), Buf("modC")
        self.phaseB_base = S.off
        ccol = S.get([8], F32); cccol = S.get([8], F32)
        screp = S.get([8, 128], F32); sccrep = S.get([8, 128], F32)
        bmr = S.get([6 * D], F32, parts=1)
        ones1 = self.ones_f[0:1, :]
        n1b = S.get([D], F32); n2b = S.get([D], F32)
        stg = [S.get([8, 512], F32) for _ in range(2)]
        Bst = [Buf("stg0"), Buf("stg1")]
        Bcc, Brep, Bbm, Bn = Buf("ccol"), Buf("rep"), Buf("bmr"), Buf("nb")
        for k in range(8):
            self.dma(ccol[:, k:k + 1], self.c[k * 128:(k + 1) * 128].rearrange("(p o) -> p o", o=1), writes=[Bcc])
            self.dma(cccol[:, k:k + 1], self.c_ctx[k * 128:(k + 1) * 128].rearrange("(p o) -> p o", o=1), writes=[Bcc])
        self.dma(bmr, self.b_mod.rearrange("(o n) -> o n", o=1), writes=[Bbm])
        self.dma(n1b, self.norm1_g.partition_broadcast(128), writes=[Bn])
        self.dma(n2b, self.norm2_g.partition_broadcast(128), writes=[Bn])
        self.A([Bcc], [Bcc], ccol, ccol, AF.Silu)
        self.A([Bcc], [Bcc], cccol, cccol, AF.Silu)
        self.V("tensor_copy", [Bcc], [Brep], out=screp, in_=ccol.unsqueeze(2).to_broadcast([128, 8, 128]))
        self.V("tensor_copy", [Bcc], [Brep], out=sccrep, in_=cccol.unsqueeze(2).to_broadcast([128, 8, 128]))
        Bps = [Buf(f"psA{i}") for i in range(4)]
        wm = self.w_mod.rearrange("(k p) n -> p k n", p=128)
        for g in range(12):
            s = g % 2
            self.dma(stg[s], wm[:, :, g * 512:(g + 1) * 512], writes=[Bst[s]])
            for which in range(2 if g < 4 else 1):
                rep = screp if which == 0 else sccrep
                pb = self.pbank(2 * which + s)
                Bp = Bps[2 * which + s]
                for dc in range(8):
                    self.MM([Brep, Bst[s], self.Bc], [Bp], pb, rep[:, dc, :], stg[s][:, dc, :], dc == 0, False, inc=False)
                self.MM([Bbm, self.Bc], [Bp], pb, ones1, bmr[0:1, g * 512:(g + 1) * 512], False, True)
                if which == 1:
                    dst, Bd = self.modC[:, g * 512:(g + 1) * 512], self.BmodC
                elif g < 4:
                    dst, Bd = self.modA[:, g * 512:(g + 1) * 512], self.BmodA
                else:
                    dst, Bd = self.modp[:, (g - 4) * 512:(g - 3) * 512], self.Bmodp
                self.A([Bp], [Bd], dst, pb, AF.Copy)
        for m, Bm, nb in ((self.modA, self.BmodA, n1b), (self.modC, self.BmodC, n1b)):
            self.V("scalar_tensor_tensor", [Bm, Bn], [Bm], out=m[:, D:2 * D], in0=m[:, D:2 * D], scalar=1.0, in1=nb, op0=ALU.add, op1=ALU.mult)
        self.V("scalar_tensor_tensor", [self.Bmodp, Bn], [self.Bmodp], out=self.modp[:, 2 * D:3 * D], in0=self.modp[:, 2 * D:3 * D], scalar=1.0, in1=n2b, op0=ALU.add, op1=ALU.mult)
        self.tap("modA", self.modA, self.BmodA)
        self.tap("modC", self.modC, self.BmodC)
        self.tap("modp", self.modp, self.Bmodp)
        f.fence()

    def phaseB(self):
        f = self.f
        self.hT = Alloc(self.arena, 54 * KBY, 90 * KBY).get([8, NK * 128], BF16)
        self.BhT = [Buf(f"hT{t}") for t in range(NK)]
        S = Alloc(self.arena, self.phaseB_base, 207 * KBY)
        xin = [S.get([D], F32) for _ in range(2)]
        tmp = [S.get([D], F32) for _ in range(2)]
        hb = [S.get([D], BF16) for _ in range(2)]
        junk = S.get([D], F32)
        ss = S.get([NK], F32); vv = S.get([NK], F32); rstd = S.get([NK], F32)
        Bx = [Buf(), Buf()]; Bt = [Buf(), Buf()]; Bh = [Buf(), Buf()]; Bj = Buf(); Bss = [Buf() for _ in range(NK)]
        Bps = [Buf(), Buf()]
        def b_stage1(tt):
            s = tt % 2
            src = self.ctx[tt * 128:(tt + 1) * 128, :] if tt < NTC else self.x[(tt - NTC) * 128:(tt - NTC + 1) * 128, :]
            self.dma(xin[s], src, writes=[Bx[s]])
            self.A([Bx[s]], [Bj, Bss[tt]], junk, xin[s], AF.Square, accum_out=ss[:, tt:tt + 1])
            self.rsqrt_chain(ss[:, tt:tt + 1], rstd[:, tt:tt + 1], vv[:, tt:tt + 1], 1.0 / D, Bss[tt], Bss[tt], Bss[tt])

        def b_stage2(tt):
            s = tt % 2
            mod, Bm = (self.modC, self.BmodC) if tt < NTC else (self.modA, self.BmodA)
            self.V("scalar_tensor_tensor", [Bx[s], Bss[tt], Bm], [Bt[s]], out=tmp[s], in0=xin[s], scalar=rstd[:, tt:tt + 1], in1=mod[:, D:2 * D], op0=ALU.mult, op1=ALU.mult)
            self.G("tensor_tensor", [Bt[s], Bm], [Bh[s]], out=hb[s], in0=tmp[s], in1=mod[:, 0:D], op=ALU.add)
            pt = self.pbank(s, BF16).rearrange("p (k t) -> p k t", k=8)
            for dc in range(8):
                self.TR([Bh[s], self.Bc], [Bps[s]], pt[:, dc, :], hb[s][:, dc * 128:(dc + 1) * 128], self.ident_b, inc=(dc == 7))
            self.A([Bps[s]], [self.BhT[tt]], self.hT[:, :, tt * 128:(tt + 1) * 128], pt, AF.Copy)

        b_stage1(0)
        for tt in range(NK):
            if tt + 1 < NK:
                b_stage1(tt + 1)
            b_stage2(tt)
        self.tap("hT", self.hT, self.BhT[NK - 1])
        f.fence()

    def load_weight_bf16(self, dst, Bdst, src, c0, ncols, S, chunk=256, cast_engs=("gpsimd", "scalar"), stg=None, Bst=None):
        if stg is None:
            stg = [S.get([8, chunk], F32) for _ in range(2)]
            Bst = [Buf(), Buf()]
        srcv = src.rearrange("(k p) n -> p k n", p=128)
        for i in range(ncols // chunk):
            s = i % 2
            self.dma(stg[s], srcv[:, :, c0 + i * chunk:c0 + (i + 1) * chunk], writes=[Bst[s]])
            eng = cast_engs[i % len(cast_engs)]
            if eng == "scalar":
                self.A([Bst[s]], [Bdst], dst[:, :, i * chunk:(i + 1) * chunk], stg[s], AF.Copy)
            else:
                self.E(eng, "tensor_copy", [Bst[s]], [Bdst], out=dst[:, :, i * chunk:(i + 1) * chunk], in_=stg[s])

    def rope(self, eng_a, eng_b, src, Bsrc, dst, Bdst, t1, t2, Bt1, Bt2, cs, sn, Bcs, nblk):
        v3 = lambda a: a.rearrange("p (n d) -> p n d", n=nblk)
        v5 = lambda a: a.rearrange("p (n g s q) -> p n g s q", n=nblk, g=2, s=2)
        cb = cs.unsqueeze(1).to_broadcast([128, nblk, 64])
        s5 = sn.rearrange("p (g s q) -> p g s q", g=2, s=2).unsqueeze(1).to_broadcast([128, nblk, 2, 2, 16])
        self.E(eng_a, "tensor_tensor", [Bsrc, Bcs], [Bt1], out=v3(t1), in0=v3(src), in1=cb, op=ALU.mult)
        self.E(eng_a, "tensor_tensor", [Bsrc, Bcs], [Bt2], out=v5(t2)[:, :, :, 0, :], in0=v5(src)[:, :, :, 1, :], in1=s5[:, :, :, 0, :], op=ALU.mult)
        self.E(eng_a, "tensor_tensor", [Bsrc, Bcs], [Bt2], out=v5(t2)[:, :, :, 1, :], in0=v5(src)[:, :, :, 0, :], in1=s5[:, :, :, 1, :], op=ALU.mult)
        self.E(eng_b, "tensor_tensor", [Bt1, Bt2], [Bdst], out=dst, in0=t1, in1=t2, op=ALU.add)

    def phaseC(self):
        f = self.f
        S = Alloc(self.arena, 90 * KBY, 207 * KBY)
        wb = S.get([8, 1536], BF16); Bwb = Buf("wbC")
        self.load_weight_bf16(wb, Bwb, self.w_in, 0, 1536, S)
        O = Alloc(self.arena, 130 * KBY, 189 * KBY)
        self.qT_ret = O.get([4, L], BF16); self.kT_ret = O.get([2, NK * 128], BF16)
        self.v_ret = O.get([NK, 512], BF16); self.rgT = O.get([4, L], BF16)
        self.BqT_ret, self.BkT_ret, self.Bv_ret, self.BrgT = Buf("qTr"), Buf("kTr"), Buf("vr"), Buf("rgT")
        T = Alloc(self.arena, 189 * KBY, 207 * KBY)
        self.V("memset", [], [self.BqT_ret], self.qT_ret, 0.0)
        qTv = self.qT_ret.rearrange("p (c j) t -> p c j t", j=2)
        t1 = [T.get([512], F32) for _ in range(2)]; t2 = [T.get([512], F32) for _ in range(2)]
        qkr = [T.get([512], BF16) for _ in range(2)]
        cs = [T.get([64], F32) for _ in range(2)]; sn = [T.get([64], F32) for _ in range(2)]
        Bt1 = [Buf(), Buf()]; Bt2 = [Buf(), Buf()]; Bqkr = [Buf(), Buf()]; Bcs = [Buf(), Buf()]
        Bp = [Buf() for _ in range(8)]
        def c_stage1(tt):
            s = tt % 2
            lat = tt >= NTC
            hs = self.hT[:, :, tt * 128:(tt + 1) * 128]
            pq = self.pbank(0 + s)
            for dc in range(8):
                self.MM([self.BhT[tt], Bwb], [Bp[0 + s]], pq, hs[:, dc, :], wb[:, dc, 0:512], dc == 0, dc == 7, inc=(dc == 7))
            pv = self.pbank(4 + s)
            for dc in range(8):
                self.MM([self.BhT[tt], Bwb], [Bp[4 + s]], pv, hs[:, dc, :], wb[:, dc, 512:1024], dc == 0, dc == 7, inc=(dc == 7))
            self.A([Bp[4 + s]], [self.Bv_ret], self.v_ret[:, tt, :], pv, AF.Copy)
            if lat:
                l0 = (tt - NTC) * 128
                self.dma(cs[s], self.rcos[l0:l0 + 128, :], writes=[Bcs[s]])
                self.dma(sn[s], self.rsin[l0:l0 + 128, :], writes=[Bcs[s]])
                self.rope("vector", "gpsimd", pq, Bp[0 + s], qkr[s], Bqkr[s], t1[s], t2[s], Bt1[s], Bt2[s], cs[s], sn[s], Bcs[s], 8)
            else:
                self.A([Bp[0 + s]], [Bqkr[s]], qkr[s], pq, AF.Copy)

        def c_stage2(tt):
            s = tt % 2
            lat = tt >= NTC
            pt = self.pbank(2 + s, BF16).rearrange("p (k t) -> p k t", k=8)
            cl = (0, 1, 2, 3) if lat else (2, 3)
            for c in cl:
                self.TR([Bqkr[s], self.Bc], [Bp[2 + s]], pt[:, c, :], qkr[s][:, c * 128:(c + 1) * 128], self.ident_b, inc=(c == 3))
            if lat:
                l0 = (tt - NTC) * 128
                self.A([Bp[2 + s]], [self.BqT_ret], qTv[0:64, :, 0, l0:l0 + 128], pt[0:64, 0:2, :], AF.Copy)
                self.A([Bp[2 + s]], [self.BqT_ret], qTv[64:128, :, 1, l0:l0 + 128], pt[64:128, 0:2, :], AF.Copy)
            self.A([Bp[2 + s]], [self.BkT_ret], self.kT_ret[:, :, tt * 128:(tt + 1) * 128], pt[:, 2:4, :], AF.Copy)

        c_stage1(0)
        for tt in range(NK):
            if tt + 1 < NK:
                c_stage1(tt + 1)
            c_stage2(tt)
        cnt = 0
        for g in range(4):
            for c in range(4):
                s = cnt % 2; cnt += 1
                pg = self.pbank(6 + s)
                for dc in range(8):
                    self.MM(self.BhT + [Bwb], [Bp[6 + s]], pg, wb[:, dc, 1024 + c * 128:1024 + (c + 1) * 128],
                            self.hT[:, dc, LC + g * 512:LC + (g + 1) * 512], dc == 0, dc == 7, inc=(dc == 7))
                self.A([Bp[6 + s]], [self.BrgT], self.rgT[:, c, g * 512:(g + 1) * 512], pg, AF.Silu)
        self.tap("qT_ret", self.qT_ret, self.BqT_ret)
        self.tap("kT_ret", self.kT_ret, self.BkT_ret)
        self.tap("v_ret", self.v_ret, self.Bv_ret)
        self.tap("rgT", self.rgT, self.BrgT)
        f.fence()

    def headnorm(self, y, By, sq, Bsq, pn, Bpn, rs, Brs, tmp, Btmp):
        self.A([By], [Bsq], sq, y, AF.Square)
        self.MM([Bsq, self.Bc], [Bpn], pn, self.ones_f, sq, True, True)
        self.rsqrt_chain(pn, rs, tmp, 1.0 / 128, Bpn, Brs, Btmp)

    def phaseD(self):
        f = self.f
        self.catT = Alloc(self.arena, 22 * KBY, 54 * KBY).get([8, L], BF16)
        self.BcatT = Buf("catT")
        S = Alloc(self.arena, 90 * KBY, 130 * KBY)
        W = 3968
        OFF = 1920
        Mstrip = S.get([W], BF16); Mc = S.get([2, L], BF16)
        T32 = S.get([L], F32)
        T = Alloc(self.arena, 189 * KBY, 207 * KBY)
        tmpb = T.get([L], F32)
        Y = Alloc(self.arena, S.off, 130 * KBY)
        NS = 5
        SBK = (0, 1, 2, 6, 7)
        PT = [Y.get([512], BF16) for _ in range(NS)]
        Ssb = [Y.get([512], BF16) for _ in range(NS)]
        BT32 = Buf("T32"); BSsb = [Buf() for _ in range(NS)]
        ysb = T.get([512], F32); sq = T.get([512], F32); rs = T.get([512], F32); tm = T.get([512], F32)
        BM, BMc, Btmpb = Buf("Mstrip"), Buf("Mc"), Buf("tmpb")
        BPT = [Buf() for _ in range(NS)]
        Bps = [Buf() for _ in range(8)]
        By, Bsq, Brs, Btm = Buf(), Buf(), Buf(), Buf()
        its = [(h, g, kt) for h in range(4) for g in range(4) for kt in range(NK)]
        LA = NS - 1

        def maskgen(h):
            lgf = self.lg[:, h:h + 1]; lgb = self.lg[:, 4 + h:5 + h]; nlgb = self.nlgb[:, h:h + 1]
            HW = W // 2
            for hf in range(2):
                dl = tmpb[:, 0:HW]
                ms = T32[:, 0:HW]
                self.G("iota", [], [Btmpb], dl, pattern=[[1, HW]], base=-OFF + hf * HW, channel_multiplier=-1, allow_small_or_imprecise_dtypes=True)
                self.V("tensor_scalar", [Btmpb, self.Bsmall], [BT32], out=ms, in0=dl, scalar1=0.0, scalar2=lgf, op0=ALU.max, op1=ALU.mult)
                self.V("tensor_scalar", [Btmpb, self.Bsmall], [Btmpb], out=dl, in0=dl, scalar1=0.0, scalar2=nlgb, op0=ALU.min, op1=ALU.mult)
                self.V("tensor_tensor", [Btmpb, BT32], [BT32], out=ms, in0=ms, in1=dl, op=ALU.add)
                self.A([BT32], [BM], Mstrip[:, hf * HW:(hf + 1) * HW], ms, AF.Exp)
            self.G("iota", [], [BT32], T32, pattern=[[1, L]], base=LC, channel_multiplier=-1, allow_small_or_imprecise_dtypes=True)
            self.G("iota", [], [Btmpb], tmpb, pattern=[[-1, L]], base=L, channel_multiplier=1, allow_small_or_imprecise_dtypes=True)
            self.A([BT32, self.Bsmall], [BT32], T32, T32, AF.Exp, scale=lgf)
            self.A([Btmpb, self.Bsmall], [Btmpb], tmpb, tmpb, AF.Exp, scale=lgb)
            self.V("tensor_tensor", [Btmpb, BT32], [BMc], out=Mc[:, 0, :], in0=T32, in1=tmpb, op=ALU.add)
            self.V("tensor_scalar", [BT32, self.Bsmall], [BT32], out=T32, in0=T32, scalar1=self.cshift[:, h:h + 1], scalar2=None, op0=ALU.mult)
            self.V("scalar_tensor_tensor", [Btmpb, BT32, self.Bsmall], [BMc], out=Mc[:, 1, :], in0=tmpb, scalar=self.cshift[:, 4 + h:5 + h], in1=T32, op0=ALU.mult, op1=ALU.add)

        def emitS(i):
            h, g, kt = its[i]
            c = h // 2
            s = i % NS
            self.MM([self.BkT_ret, self.BqT_ret], [Bps[SBK[s]]], self.pbank(SBK[s]), self.kT_ret[:, c, kt * 128:(kt + 1) * 128],
                    self.qT_ret[:, h, g * 512:(g + 1) * 512], True, True)

        def emitRest(i):
            h, g, kt = its[i]
            s = i % NS
            ps = self.pbank(SBK[s])
            ob = 3 + (h * 4 + g) % 2
            po, Bpo = self.pbank(ob), Bps[ob]
            if g == 0 and kt == 0:
                maskgen(h)
                if h == 0:
                    self.convert_tables(0, 6)
            if kt < NTC:
                mk, Bmk = Mc[:, kt, g * 512:(g + 1) * 512], BMc
            else:
                st = g * 512 - (kt - NTC) * 128 + OFF
                mk, Bmk = Mstrip[:, st:st + 512], BM
            self.A([Bps[SBK[s]]], [BSsb[s]], Ssb[s], ps, AF.Copy, scale=0.125)
            self.V("tensor_tensor", [BSsb[s], Bmk], [BPT[s]], out=PT[s], in0=Ssb[s], in1=mk, op=ALU.mult)
            self.MM([self.Bv_ret, BPT[s]], [Bpo], po, self.v_ret[:, kt, h * 128:(h + 1) * 128], PT[s], kt == 0, kt == NK - 1, inc=(kt == NK - 1))
            if kt == NK - 1:
                self.A([Bpo], [By], ysb, po, AF.Copy)
                self.A([By], [Bsq], sq, ysb, AF.Square)

                def fin(h=h, g=g):
                    self.MM([Bsq, self.Bc], [Bps[5]], self.pbank(5), self.ones_f, sq, True, True)
                    self.rsqrt_lnexp(self.pbank(5), rs, tm, 1.0 / 128, Bps[5], Brs, Btm)
                    self.V("tensor_tensor", [By, Brs], [Btm], out=tm, in0=ysb, in1=rs, op=ALU.mult)
                    self.V("scalar_tensor_tensor", [Btm, self.Bsmall, self.BrgT], [self.BcatT], out=self.catT[:, h, g * 512:(g + 1) * 512],
                           in0=tm, scalar=self.rg_col[:, h:h + 1], in1=self.rgT[:, h, g * 512:(g + 1) * 512], op0=ALU.mult, op1=ALU.mult)
                pending.append((i + DEFER, fin))

        pending = []
        DEFER = 8
        for i in range(len(its) + LA):
            if i < len(its):
                emitS(i)
            if i - LA >= 0:
                emitRest(i - LA)
                while pending and pending[0][0] <= i - LA:
                    pending.pop(0)[1]()
        for _, fn in pending:
            fn()
        if self.stop_after == "D":
            self.tap("catT", self.catT, self.BcatT)
        f.fence(skip=self.cv_clocks)

    def phaseE(self):
        f = self.f
        S = Alloc(self.arena, 90 * KBY, 130 * KBY)
        wb = S.get([8, 1536], BF16); Bwb = Buf("wbE")
        self.load_weight_bf16(wb, Bwb, self.w_in, 1536, 1536, S, chunk=64)
        O = Alloc(self.arena, 130 * KBY, 198 * KBY)
        self.qT_diff = O.get([8, L], BF16); self.kT_diff = O.get([4, NK * 128], BF16)
        self.v_diff = O.get([NK, 512], BF16)
        self.BqT_diff, self.BkT_diff, self.Bv_diff = Buf("qTd"), Buf("kTd"), Buf("vd")
        T = Alloc(self.arena, 198 * KBY, 207 * KBY)
        X = Alloc(self.arena, 38 * KBY, 54 * KBY)
        qk = [S.get([1024], F32), X.get([1024], F32)]; xa = [S.get([1024], F32), X.get([1024], F32)]
        xb = [S.get([1024], F32), X.get([1024], F32)]
        qkr = [T.get([1024], BF16) for _ in range(2)]
        cs = [T.get([64], F32) for _ in range(2)]; sn = [T.get([64], F32) for _ in range(2)]
        ss = [T.get([16], F32) for _ in range(2)]; vv = [T.get([16], F32) for _ in range(2)]; rstd = [T.get([16], F32) for _ in range(2)]
        Bqk, Bxa, Bxb = [Buf(), Buf()], [Buf(), Buf()], [Buf(), Buf()]
        Bqkr = [Buf(), Buf()]; Bcs = [Buf(), Buf()]; Bss = [Buf(), Buf()]
        self.V("memset", [], [self.BqT_diff], self.qT_diff, 0.0)
        qTv = self.qT_diff.rearrange("p (h j) t -> p h j t", j=2)
        Bp = [Buf() for _ in range(8)]
        v16 = lambda a: a.rearrange("p (n d) -> p n d", n=16)
        def stage1(tt):
            s = tt % 2
            hs = self.hT[:, :, tt * 128:(tt + 1) * 128]
            for half in range(2):
                b = (0, 1)[half] if s == 0 else (6, 7)[half]
                pq = self.pbank(b)
                for dc in range(8):
                    self.MM([self.BhT[tt], Bwb], [Bp[b]], pq, hs[:, dc, :], wb[:, dc, half * 512:(half + 1) * 512], dc == 0, dc == 7, inc=(dc == 7))
                self.A([Bp[b]], [Bqk[s]], qk[s][:, half * 512:(half + 1) * 512], pq, AF.Copy)
            pv = self.pbank(4 + s)
            for dc in range(8):
                self.MM([self.BhT[tt], Bwb], [Bp[4 + s]], pv, hs[:, dc, :], wb[:, dc, 1024:1536], dc == 0, dc == 7, inc=(dc == 7))
            self.A([Bp[4 + s]], [self.Bv_diff], self.v_diff[:, tt, :], pv, AF.Copy)
            self.V("tensor_tensor", [Bqk[s]], [Bxa[s]], out=xa[s], in0=qk[s], in1=qk[s], op=ALU.mult)
            self.V("tensor_reduce", [Bxa[s]], [Bss[s]], out=ss[s], in_=v16(xa[s]), axis=AX.X, op=ALU.add)
            self.rsqrt_chain(ss[s], rstd[s], vv[s], 1.0 / 64, Bss[s], Bss[s], Bss[s])
            self.V("tensor_tensor", [Bqk[s], Bss[s]], [Bxa[s]], out=v16(xa[s]), in0=v16(qk[s]), in1=rstd[s].unsqueeze(2).to_broadcast([128, 16, 64]), op=ALU.mult)

        def stage2(tt):
            s = tt % 2
            lat = tt >= NTC
            if lat:
                l0 = (tt - NTC) * 128
                self.dma(cs[s], self.rcos[l0:l0 + 128, :], writes=[Bcs[s]])
                self.dma(sn[s], self.rsin[l0:l0 + 128, :], writes=[Bcs[s]])
                self.G("tensor_tensor", [Bxa[s], self.Bsmall], [Bxb[s]], out=xb[s], in0=xa[s], in1=self.gqk, op=ALU.mult)
                self.rope("vector", "gpsimd", xb[s], Bxb[s], qkr[s], Bqkr[s], qk[s], xa[s], Bqk[s], Bxa[s], cs[s], sn[s], Bcs[s], 16)
            else:
                self.G("tensor_tensor", [Bxa[s], self.Bsmall], [Bqkr[s]], out=qkr[s], in0=xa[s], in1=self.gqk, op=ALU.mult)
            ptq = self.pbank(2, BF16).rearrange("p (k t) -> p k t", k=8)
            ptk = self.pbank(3, BF16).rearrange("p (k t) -> p k t", k=8)
            if lat:
                for c in range(4):
                    self.TR([Bqkr[s], self.Bc], [Bp[2]], ptq[:, c, :], qkr[s][:, c * 128:(c + 1) * 128], self.ident_b, inc=(c == 3))
                self.A([Bp[2]], [self.BqT_diff], qTv[0:64, :, 0, l0:l0 + 128], ptq[0:64, 0:4, :], AF.Copy)
                self.A([Bp[2]], [self.BqT_diff], qTv[64:128, :, 1, l0:l0 + 128], ptq[64:128, 0:4, :], AF.Copy)
            for c in range(4):
                self.TR([Bqkr[s], self.Bc], [Bp[3]], ptk[:, c, :], qkr[s][:, 512 + c * 128:512 + (c + 1) * 128], self.ident_b, inc=(c == 3))
            self.A([Bp[3]], [self.BkT_diff], self.kT_diff[:, :, tt * 128:(tt + 1) * 128], ptk[:, 0:4, :], AF.Copy)

        stage1(0)
        for tt in range(NK):
            if tt + 1 < NK:
                stage1(tt + 1)
            stage2(tt)
        self.tap("qT_diff", self.qT_diff, self.BqT_diff)
        self.tap("kT_diff", self.kT_diff, self.BkT_diff)
        self.tap("v_diff", self.v_diff, self.Bv_diff)
        f.fence()

    def phaseF(self):
        f = self.f
        self.convert_tables(6, 16)
        S = Alloc(self.arena, 90 * KBY, 130 * KBY)
        ET = [S.get([512], BF16) for _ in range(3)]
        rz = [S.get([512], F32) for _ in range(2)]
        ts = [S.get([512], F32) for _ in range(2)]
        ysb = S.get([512], F32); sq = S.get([512], F32); rs = S.get([512], F32); tm = S.get([512], F32)
        BET = [Buf() for _ in range(3)]
        Brz = [Buf(), Buf()]; Bts = [Buf(), Buf()]
        By, Bsq, Brs, Btm = Buf(), Buf(), Buf(), Buf()
        Bps = [Buf() for _ in range(8)]
        its = [(h, g, sub, kt) for h in range(4) for g in range(4) for sub in range(2) for kt in range(NK)]
        LA = 2

        def emitS(i):
            h, g, sub, kt = its[i]
            pb = sub * 64
            s = i % 3
            self.MM([self.BkT_diff, self.BqT_diff], [Bps[s]], self.pbank(s), self.kT_diff[:, h, kt * 128:(kt + 1) * 128],
                    self.qT_diff[:, h * 2 + sub, g * 512:(g + 1) * 512], True, True)

        def emitRest(i):
            h, g, sub, kt = its[i]
            s = i % 3
            ps = self.pbank(s)
            po = self.pbank(3 + sub); Bpo = Bps[3 + sub]
            pz = self.pbank(5 + sub); Bpz = Bps[5 + sub]
            self.A([Bps[s]], [BET[s]], ET[s], ps, AF.Exp, scale=0.125)
            last = kt == NK - 1
            self.MM([self.Bv_diff, BET[s]], [Bpo], po, self.v_diff[:, kt, h * 128:(h + 1) * 128], ET[s], kt == 0, last, inc=False)
            self.MM([self.Bc, BET[s]], [Bpz], pz, self.ones_b, ET[s], kt == 0, last, inc=last)
            if last:
                self.V("reciprocal", [Bpz], [Brz[sub]], out=rz[sub], in_=pz)
                self.V("tensor_tensor", [Bpo, Brz[sub]], [Bts[sub]], out=ts[sub], in0=po, in1=rz[sub], op=ALU.mult)
            if last and sub == 1:
                self.V("scalar_tensor_tensor", [Bts[0], Bts[1], self.Bsmall], [By], out=ysb, in0=ts[1], scalar=self.nlam[:, 0:1], in1=ts[0], op0=ALU.mult, op1=ALU.add)

                def fin(h=h, g=g):
                    self.A([By], [Bsq], sq, ysb, AF.Square)
                    self.MM([Bsq, self.Bc], [Bps[7]], self.pbank(7), self.ones_f, sq, True, True)
                    self.rsqrt_lnexp(self.pbank(7), rs, tm, 1.0 / 128, Bps[7], Brs, Btm)
                    self.V("tensor_tensor", [By, Brs], [Btm], out=tm, in0=ysb, in1=rs, op=ALU.mult)
                    self.V("tensor_scalar", [Btm, self.Bsmall], [self.BcatT], out=self.catT[:, 4 + h, g * 512:(g + 1) * 512],
                           in0=tm, scalar1=self.dg_col[:, h:h + 1], scalar2=None, op0=ALU.mult)
                pending.append((i + DEFER, fin))

        pending = []
        DEFER = 10
        for i in range(len(its) + LA):
            if i < len(its):
                emitS(i)
            if i - LA >= 0:
                emitRest(i - LA)
                while pending and pending[0][0] <= i - LA:
                    pending.pop(0)[1]()
        for _, fn in pending:
            fn()
        self.tap("catT", self.catT, self.BcatT)
        f.fence(skip=self.cv_clocks)


    def convert_tables(self, lo, hi):
        R = 2048
        jobs = [(c, tab, off) for c in range(NEXP // R) for tab, off in ((self.pu, 0), (self.pv, D))]
        for n in range(lo, hi):
            c, tab, off = jobs[n]
            clk = self.f.dma_clock(f"cv{n}")
            self.cv_clocks.append(clk)
            self.dma(self.puv[c * R:(c + 1) * R, off:off + D], tab[c * R:(c + 1) * R, :], writes=[self.Bpuv], eng="gpsimd", clock=clk)

    def phaseG(self):
        f = self.f
        self.x1 = Alloc(self.arena, 54 * KBY, 118 * KBY).get([NT, D], F32)
        self.Bx1 = [Buf(f"x1_{t}") for t in range(NT)]
        S = Alloc(self.arena, 118 * KBY, 207 * KBY)
        wo = S.get([8, D], BF16); Bwo = Buf("wo")
        self.load_weight_bf16(wo, Bwo, self.w_out, 0, D, S, cast_engs=("vector", "scalar"))
        xin = [S.get([D], F32) for _ in range(2)]; tmp = [S.get([D], F32) for _ in range(2)]
        Bx = [Buf(), Buf()]; Bt = [Buf(), Buf()]
        Bp = [Buf() for _ in range(8)]
        gate1 = self.modp[:, 0:D]
        for tt in range(NT):
            s = tt % 2
            self.dma(xin[s], self.x[tt * 128:(tt + 1) * 128, :], writes=[Bx[s]])
            for half in range(2):
                b = 2 * s + half
                pb = self.pbank(b)
                for kc in range(8):
                    self.MM([self.BcatT, Bwo], [Bp[b]], pb, self.catT[:, kc, tt * 128:(tt + 1) * 128], wo[:, kc, half * 512:(half + 1) * 512], kc == 0, kc == 7, inc=(kc == 7))
                self.V("tensor_tensor", [Bp[b], self.Bmodp], [Bt[s]], out=tmp[s][:, half * 512:(half + 1) * 512], in0=pb, in1=gate1[:, half * 512:(half + 1) * 512], op=ALU.mult)
            self.V("tensor_tensor", [Bt[s], Bx[s]], [self.Bx1[tt]], out=self.x1[:, tt, :], in0=tmp[s], in1=xin[s], op=ALU.add)
        self.tap("x1", self.x1, self.Bx1[NT - 1])
        f.fence(skip=getattr(self, "cv_clocks", ()))

    def phaseH(self):
        f = self.f
        NB = 8
        GK = 2
        wqb = Alloc(self.arena, 22 * KBY, 54 * KBY).get([8, 2048], BF16); Bwq = Buf("wq")
        S = Alloc(self.arena, 118 * KBY, 207 * KBY)
        skT = S.get([16, 128], BF16); BskT = Buf("skT")
        h2b = [S.get([D], BF16) for _ in range(2)]; h2T = S.get([8, 128], BF16)
        tmp = S.get([D], F32); qTs = S.get([16, 128], BF16); junk = S.get([D], BF16)
        SA = S.get([2048], F32); SB = S.get([2048], F32)
        v16 = S.get([16, 16], F32); ix = S.get([16, 16], U32); ixf = S.get([16, 16], F32)
        best = S.get([8, 16], F32); pos = S.get([8, 16], U32); pa = S.get([128], U32); pbb = S.get([128], U32)
        paf = S.get([8, 16], F32); pbf = S.get([8, 16], F32); i1s = S.get([8, 16], F32); i2s = S.get([8, 16], F32)
        eif = S.get([128], F32); eidx = [S.get([128], I32) for _ in range(2)]
        gexp = S.get([8, 16], F32); gsum = S.get([8], F32); gate = [S.get([128], F32) for _ in range(2)]
        dots = [S.get([128], F32) for _ in range(2)]; wgt = [S.get([128], F32) for _ in range(2)]
        iotaK = S.get([16, 16], F32); ss = S.get([NT], F32); vv = S.get([NT], F32); rstd = S.get([NT], F32)
        ring = [S.get([2 * D], BF16) for _ in range(NB)]
        vb = [S.get([D], BF16) for _ in range(2)]
        dgw = [vb[0][:, 0:128], vb[0][:, 128:256], vb[0][:, 256:384], vb[0][:, 384:512]]
        Bdg = [Buf() for _ in range(4)]
        Bh2 = [Buf(), Buf()]; Bh2T, Btmp, BqTs, BSA, BSB, Bjunk = Buf(), Buf(), Buf(), Buf(), Buf(), Buf()
        Brt = Buf("route"); Beidx = [Buf(), Buf()]; Bgate = [Buf(), Buf()]
        Bdots = [[Buf() for _ in range(128 // GK)] for _ in range(2)]; Bw = [[Buf() for _ in range(128 // GK)] for _ in range(2)]
        Bio, Bss = Buf(), Buf()
        Brg = [Buf() for _ in range(NB)]; Bvb = [Buf(), Buf()]
        gclk = [f.dma_clock(f"pg{i}") for i in range(NB)]
        Bp = [Buf() for _ in range(8)]
        stgv = [ring[i].bitcast(F32).rearrange("p (a b) -> p a b", a=8) for i in range(2)]
        self.load_weight_bf16(wqb, Bwq, self.wq, 0, 2048, S, chunk=128, stg=stgv, Bst=[Brg[0], Brg[1]], cast_engs=("vector", "scalar"))
        S2 = self.modp[:, D:2 * D]; A2 = self.modp[:, 2 * D:3 * D]; gate2 = self.modp[:, 3 * D:4 * D]
        skst = [ring[2 + i].bitcast(F32).rearrange("p (a b) -> p a b", a=8) for i in range(2)]
        skb = vb[0].rearrange("p (a b) -> p a b", a=8)
        Bsk = Buf()
        for half in range(2):
            self.dma(skst[half], self.sk[half * 8:(half + 1) * 8].rearrange("h n c -> n h c"), writes=[Brg[2 + half]])
            self.V("tensor_copy", [Brg[2 + half]], [Bsk], out=skb, in_=skst[half])
            pt = self.pbank(0, BF16).rearrange("p (k t) -> p k t", k=8)
            for k in range(8):
                self.TR([Bsk, self.Bc], [Bp[0]], pt[:, k, :], skb[:, k, :], self.ident_b, inc=(k == 7))
            self.A([Bp[0]], [BskT], skT[:, half * 8:(half + 1) * 8, :], pt, AF.Copy)
        for a in range(16):
            self.V("memset", [], [Bio], iotaK[:, :, a:a + 1], float(a))
        self.V("memset", [], [Bss], ss, 0.0)
        from collections import deque
        free = deque(range(NB))

        def add_slot(j):
            ring.append(self.x1[:, j, :].bitcast(BF16))
            Brg.append(self.Bx1[j])
            gclk.append(f.dma_clock(f"pgx{j}"))
            free.appendleft(len(ring) - 1)

        def route(tt):
            s = tt % 2
            xt = self.x1[:, tt, :]
            self.A([self.Bx1[tt]], [Btmp, Bss], tmp, xt, AF.Square, accum_out=ss[:, tt:tt + 1])
            self.rsqrt_lnexp(ss[:, tt:tt + 1], rstd[:, tt:tt + 1], vv[:, tt:tt + 1], 1.0 / D, Bss, Bss, Bss)
            self.V("scalar_tensor_tensor", [self.Bx1[tt], Bss, self.Bmodp], [Btmp], out=tmp, in0=xt, scalar=rstd[:, tt:tt + 1], in1=A2, op0=ALU.mult, op1=ALU.mult)
            yield
            self.V("tensor_tensor", [Btmp, self.Bmodp], [Bh2[s]], out=h2b[s], in0=tmp, in1=S2, op=ALU.add)
            pt = self.pbank(0, BF16).rearrange("p (k t) -> p k t", k=8)
            for dc in range(8):
                self.TR([Bh2[s], self.Bc], [Bp[0]], pt[:, dc, :], h2b[s][:, dc * 128:(dc + 1) * 128], self.ident_b, inc=(dc == 7))
            self.A([Bp[0]], [Bh2T], h2T, pt, AF.Copy)
            yield
            for half in range(2):
                for q4 in range(2):
                    b = 1 + q4
                    pq = self.pbank(b).rearrange("p (k t) -> p k t", k=4)
                    for j in range(4):
                        hp = half * 8 + q4 * 4 + j
                        for dc in range(8):
                            self.MM([Bh2T, Bwq], [Bp[b]], pq[:, j, :], wqb[:, dc, hp * 128:(hp + 1) * 128], h2T[:, dc, :], dc == 0, dc == 7, inc=(dc == 7 and j == 3))
                    self.A([Bp[b]], [BqTs], qTs[:, half * 8 + q4 * 4:half * 8 + q4 * 4 + 4, :], pq, AF.Copy)
                    yield
            for q4 in range(4):
                b = 3 + q4 % 2
                psx = self.pbank(b).rearrange("p (k t) -> p k t", k=4)
                for j in range(4):
                    hp = q4 * 4 + j
                    self.MM([BqTs, BskT], [Bp[b]], psx[:, j, :], qTs[:, hp, :], skT[:, hp, :], True, True, inc=(j == 3))
                self.A([Bp[b]], [BSA], SA[:, q4 * 512:(q4 + 1) * 512], self.pbank(b), AF.Copy)
            yield
            SAv = SA.rearrange("p (g n) -> p g n", g=16); SBv = SB.rearrange("p (g n) -> p g n", g=16)
            for g in range(16):
                self.V("max", [BSA], [Brt], out=v16[:, g, 0:8], in_=SAv[:, g, :])
                self.V("max_index", [BSA, Brt], [Brt], out=ix[:, g, 0:8], in_max=v16[:, g, 0:8], in_values=SAv[:, g, :])
                self.V("match_replace", [BSA, Brt], [BSB], out=SBv[:, g, :], in_to_replace=v16[:, g, 0:8], in_values=SAv[:, g, :], imm_value=-1e30)
                yield
                self.V("max", [BSB], [Brt], out=v16[:, g, 8:16], in_=SBv[:, g, :])
                self.V("max_index", [BSB, Brt], [Brt], out=ix[:, g, 8:16], in_max=v16[:, g, 8:16], in_values=SBv[:, g, :])
                yield
            self.V("tensor_copy", [Brt], [Brt], out=ixf, in_=ix)
            vv4 = v16.rearrange("p (h t) k -> p h t k", t=2)
            cand = SA.rearrange("p (h a b) -> p h a b", h=8, a=16)
            self.V("tensor_tensor", [Brt, BSA], [BSA], out=cand, in0=vv4[:, :, 0, :].unsqueeze(3).to_broadcast([128, 8, 16, 16]),
                   in1=vv4[:, :, 1, :].unsqueeze(2).to_broadcast([128, 8, 16, 16]), op=ALU.add)
            yield
            c3 = SA.rearrange("p (h n) -> p h n", h=8); c3b = SB.rearrange("p (h n) -> p h n", h=8)
            for h in range(8):
                self.V("max", [BSA], [Brt], out=best[:, h, 0:8], in_=c3[:, h, :])
                self.V("max_index", [BSA, Brt], [Brt], out=pos[:, h, 0:8], in_max=best[:, h, 0:8], in_values=c3[:, h, :])
                self.V("match_replace", [BSA, Brt], [BSB], out=c3b[:, h, :], in_to_replace=best[:, h, 0:8], in_values=c3[:, h, :], imm_value=-1e30)
                yield
                self.V("max", [BSB], [Brt], out=best[:, h, 8:16], in_=c3b[:, h, :])
                self.V("max_index", [BSB, Brt], [Brt], out=pos[:, h, 8:16], in_max=best[:, h, 8:16], in_values=c3b[:, h, :])
                yield
            posf = pos.rearrange("p h k -> p (h k)")
            self.V("tensor_single_scalar", [Brt], [Brt], out=pa, in_=posf, scalar=4, op=ALU.logical_shift_right)
            self.V("tensor_single_scalar", [Brt], [Brt], out=pbb, in_=posf, scalar=15, op=ALU.bitwise_and)
            self.V("tensor_copy", [Brt], [Brt], out=paf.rearrange("p h k -> p (h k)"), in_=pa)
            self.V("tensor_copy", [Brt], [Brt], out=pbf.rearrange("p h k -> p (h k)"), in_=pbb)
            yield
            EQ = SB.rearrange("p (h k a) -> p h k a", h=8, k=16)
            ix4 = ixf.rearrange("p (h t) k -> p h t k", t=2)
            iob = iotaK.unsqueeze(1).to_broadcast([128, 8, 16, 16])
            for which, (pf, dst) in enumerate(((paf, i1s), (pbf, i2s))):
                self.V("tensor_tensor", [Brt, Bio, BSB], [BSB], out=EQ, in0=iob, in1=pf.unsqueeze(3).to_broadcast([128, 8, 16, 16]), op=ALU.is_equal)
                yield
                self.V("tensor_tensor", [Brt, BSB], [BSB], out=EQ, in0=EQ, in1=ix4[:, :, which, :].unsqueeze(2).to_broadcast([128, 8, 16, 16]), op=ALU.mult)
                yield
                self.V("tensor_reduce", [BSB], [Brt], out=dst, in_=EQ, axis=AX.X, op=ALU.add)
                yield
            self.V("scalar_tensor_tensor", [Brt], [Brt], out=eif, in0=i1s.rearrange("p h k -> p (h k)"), scalar=128.0, in1=i2s.rearrange("p h k -> p (h k)"), op0=ALU.mult, op1=ALU.add)
            self.V("tensor_copy", [Brt], [Beidx[s]], out=eidx[s], in_=eif)
            self.V("tensor_tensor", [Brt], [Brt], out=gexp, in0=best, in1=best[:, :, 0:1].to_broadcast([128, 8, 16]), op=ALU.subtract)
            self.A([Brt], [Brt], gexp, gexp, AF.Exp)
            self.V("tensor_reduce", [Brt], [Brt], out=gsum, in_=gexp, axis=AX.X, op=ALU.add)
            self.V("reciprocal", [Brt], [Brt], out=gsum, in_=gsum)
            self.V("tensor_tensor", [Brt], [Bgate[s]], out=gate[s].rearrange("p (h k) -> p h k", h=8), in0=gexp, in1=gsum.unsqueeze(2).to_broadcast([128, 8, 16]), op=ALU.mult)
            self.V("memset", [], Bdots[s], dots[s], 0.0)
            if tt == 0 and not getattr(self.f, "dry", False):
                self.tap("eidx0", eidx[0], Beidx[0])
                self.tap("gate0", gate[0], Bgate[0])

        def experts(tt, bg):
            s = tt % 2
            ngrp = 128 // GK
            slots = {}

            def issue(grp):
                for j in range(GK):
                    hk = grp * GK + j
                    r = free.popleft()
                    free.append(r)
                    slots[hk] = r
                    self.f.op("gpsimd", (lambda r=r, hk=hk, s=s: lambda e: e.indirect_dma_start(
                        out=ring[r], out_offset=None, in_=self.puv,
                        in_offset=bass.IndirectOffsetOnAxis(ap=eidx[s][:, hk:hk + 1], axis=0)))(),
                        reads=[Beidx[s], self.Bpuv], writes=[Brg[r]], clock=gclk[r])

            def dots_of(grp):
                for j in range(GK):
                    hk = grp * GK + j
                    r = slots[hk]
                    self.V("scalar_tensor_tensor", [Brg[r], Bh2[s]], [Brg[r], Bdots[s][grp]], out=ring[r][:, 0:D], in0=ring[r][:, 0:D], scalar=1.0, in1=h2b[s],
                           op0=ALU.mult, op1=ALU.mult, accum_out=dots[s][:, hk:hk + 1])

            def finish_of(grp):
                sl = slice(grp * GK, (grp + 1) * GK)
                self.A([Bdots[s][grp]], [Bw[s][grp]], wgt[s][:, sl], dots[s][:, sl], AF.Gelu)
                for j in range(GK):
                    hk = grp * GK + j
                    r = slots[hk]
                    k = hk % 4
                    self.A([Bw[s][grp], Bgate[s]], [Bw[s][grp]], wgt[s][:, hk:hk + 1], wgt[s][:, hk:hk + 1], AF.Identity, scale=gate[s][:, hk:hk + 1])
                    self.A([self.Bc, Bw[s][grp]], [Bdg[k]], dgw[k], self.ident_b, AF.Identity, scale=wgt[s][:, hk:hk + 1])
                    for half in range(2):
                        self.MM([Bdg[k], Brg[r]], [Bp[5 + half]], self.pbank(5 + half), dgw[k], ring[r][:, D + half * 512:D + (half + 1) * 512],
                                hk == 0, hk == 127, inc=(half == 1))

            LAG = min(8, len(ring) // GK - 1)
            assert LAG >= 1 and (LAG + 1) * GK <= len(ring)
            for g0 in range(LAG):
                issue(g0)
            for grp in range(ngrp):
                if grp + LAG < ngrp:
                    issue(grp + LAG)
                dots_of(grp)
                finish_of(grp)
                if bg is not None:
                    for _ in range(((grp + 1) * nyield + ngrp - 3) // (ngrp - 2) - (grp * nyield + ngrp - 3) // (ngrp - 2)):
                        next(bg, None)
            if tt == 0:
                self.tap("w0", wgt[0], Bw[0][128 // GK - 1])
            if bg is not None:
                for _ in bg:
                    pass
            for half in range(2):
                self.V("tensor_tensor", [Bp[5 + half], self.Bmodp], [Btmp], out=tmp[:, half * 512:(half + 1) * 512], in0=self.pbank(5 + half),
                       in1=gate2[:, half * 512:(half + 1) * 512], op=ALU.mult)
            self.V("tensor_tensor", [Btmp, self.Bx1[tt]], [self.Bx1[tt]], out=self.x1[:, tt, :], in0=tmp, in1=self.x1[:, tt, :], op=ALU.add)
            self.dma(self.out[tt * 128:(tt + 1) * 128, :], self.x1[:, tt, :], reads=[self.Bx1[tt]], writes=[self.Bout], clock=self.stc)

        ntiles = NT if self.stop_after != "H1" else 1
        self.f.dry = True
        nyield = sum(1 for _ in route(1)) + 1
        self.f.dry = False
        for _ in route(0):
            pass
        for tt in range(ntiles):
            experts(tt, route(tt + 1) if tt + 1 < ntiles else None)
            if tt + 2 < ntiles:
                add_slot(tt)


def rope_tables():
    quarter = 16
    freqs = (10000.0 ** (-np.arange(quarter, dtype=np.float32) / quarter)).astype(np.float32)
    rows = L // 64
    row = np.repeat(np.arange(rows, dtype=np.float32), 64)
    col = np.tile(np.arange(64, dtype=np.float32), rows)
    ar = row[:, None] * freqs
    ac = col[:, None] * freqs
    ang = np.concatenate([ar, ar, ac, ac], axis=-1).astype(np.float32)
    cos = np.cos(ang).astype(np.float32)
    sin = np.sin(ang).astype(np.float32)
    sgn = np.concatenate([-np.ones(16), np.ones(16), -np.ones(16), np.ones(16)]).astype(np.float32)
    return cos, (sin * sgn).astype(np.float32)


def make_in_map(inp, b):
    f = lambda a: np.ascontiguousarray(np.asarray(a, dtype=np.float32))
    cos, sins = rope_tables()
    return {
        "x": f(inp["x"][b]), "ctx": f(inp["ctx"][b]), "c": f(inp["c"][b]), "c_ctx": f(inp["c_ctx"]),
        "w_mod": f(inp["w_mod"][0]), "b_mod": f(inp["b_mod"][0]),
        "norm1_g": f(inp["norm1_g"][0]), "norm2_g": f(inp["norm2_g"][0]),
        "w_in": f(inp["w_in"][0]), "decay": f(inp["ret_decay_logit"][0]).reshape(8),
        "ret_norm_g": f(inp["ret_norm_g"][0]), "qk_g": f(inp["diff_qk_norm_g"][0]).reshape(128),
        "dlam": f(inp["diff_lambda"][0]).reshape(256), "diff_norm_g": f(inp["diff_norm_g"][0]),
        "w_out": f(inp["w_out"][0]), "wq": f(inp["peer_w_query"][0]),
        "sk": f(inp["peer_sub_keys"][0]).reshape(16, 128, 128),
        "pu": f(inp["peer_u"][0]), "pv": f(inp["peer_v"][0]),
        "rcos": cos, "rsin": sins,
    }


def kernel(**inputs):
    nc = Builder().build()
    in_maps = [make_in_map(inputs, b) for b in range(8)]
    res = run_bass_kernel_spmd(nc, in_maps, core_ids=list(range(8)))
    return np.stack([np.asarray(r["out"], dtype=np.float32) for r in res.results], axis=0)
```
